# Optimizing a Trainium2 kernel written in Bass

```python
import jax, jax.numpy as jnp
from jax import lax
import numpy as np

D_MODEL = 1024
BATCH = 8
SEQ = 8192
DEPTH = 1
DEC_BATCH = 8
DEC_SEQ = 64
PAST_LEN = 1024

CHUNK = 64
Q_BLOCK = 128
MLA_HEADS = 8
MLA_NOPE = 64
MLA_ROPE = 32
MLA_QK = MLA_NOPE + MLA_ROPE
MLA_V = 64
MLA_Q_RANK = 384
MLA_KV_RANK = 256
ROPE_THETA = 10000.0
GLA_HEADS = 4
GLA_DK = 64
GLA_DV = 128
GLA_GATE_RANK = 16
GLA_TAU = 16.0
MIX_WIDTH = MLA_HEADS * MLA_V + GLA_HEADS * GLA_DV
FFN_DIM = 2816
CONV_W = 3
EPS = 1e-6
COL_SIZES = (MLA_Q_RANK, MLA_KV_RANK, MLA_ROPE, GLA_HEADS * GLA_DK, GLA_HEADS * GLA_DK,
             GLA_HEADS * GLA_DV, GLA_GATE_RANK, GLA_HEADS * GLA_DV)
IN_COLS = (MLA_Q_RANK + MLA_KV_RANK + MLA_ROPE + 2 * GLA_HEADS * GLA_DK
           + GLA_HEADS * GLA_DV + GLA_GATE_RANK + GLA_HEADS * GLA_DV)

kernel_name = 'hybrid_mla_gla_convffn_stream_step'


def _rms(x, g):
    xf = x.astype(jnp.float32)
    y = xf * lax.rsqrt(jnp.mean(xf * xf, axis=-1, keepdims=True) + EPS)
    return (y * g.astype(jnp.float32)).astype(x.dtype)


def _rope(x, pos):
    half = MLA_ROPE // 2
    inv = 1.0 / (ROPE_THETA ** (jnp.arange(half, dtype=jnp.float32) / half))
    ang = pos.astype(jnp.float32)[:, None] * inv[None, :]
    cos = jnp.cos(ang)[None, :, None, :]
    sin = jnp.sin(ang)[None, :, None, :]
    xr = x[..., MLA_NOPE:].astype(jnp.float32)
    x1, x2 = xr[..., :half], xr[..., half:]
    rot = jnp.concatenate([x1 * cos - x2 * sin, x2 * cos + x1 * sin], axis=-1).astype(x.dtype)
    return jnp.concatenate([x[..., :MLA_NOPE], rot], axis=-1)


def _split_cols(p):
    out, o = [], 0
    for n in COL_SIZES:
        out.append(p[..., o:o + n])
        o += n
    return out


def _adaln(c, w_ada, b_ada):
    return jnp.split(jax.nn.silu(c) @ w_ada + b_ada, 6, axis=-1)


def _modulate(x, g, shift, scale):
    return _rms(x, g) * (1.0 + scale[:, None, :]) + shift[:, None, :]


def _mla_queries(q_lat, pos, g_qa, w_uq, g_qn):
    B, L, _ = q_lat.shape
    q = (_rms(q_lat, g_qa) @ w_uq).reshape(B, L, MLA_HEADS, MLA_QK)
    return _rope(_rms(q, g_qn), pos)


def _mla_keys(ckv, kpe, pos, w_ukv, g_kn):
    B, L, _ = ckv.shape
    kv = (ckv @ w_ukv).reshape(B, L, MLA_HEADS, MLA_NOPE + MLA_V)
    k_nope, v = kv[..., :MLA_NOPE], kv[..., MLA_NOPE:]
    k_pe = jnp.broadcast_to(kpe[:, :, None, :], (B, L, MLA_HEADS, MLA_ROPE)).astype(k_nope.dtype)
    k = jnp.concatenate([k_nope, k_pe], axis=-1)
    return _rope(_rms(k, g_kn), pos), v


def _attend(q, k, v, q_pos, k_pos):
    s = jnp.einsum('bqhd,bkhd->bhqk', q, k, preferred_element_type=jnp.float32) * (MLA_QK ** -0.5)
    visible = (k_pos[None, :] // CHUNK) <= (q_pos[:, None] // CHUNK)
    s = jnp.where(visible[None, None], s, jnp.finfo(jnp.float32).min)
    p = jax.nn.softmax(s, axis=-1).astype(v.dtype)
    return jnp.einsum('bhqk,bkhd->bqhd', p, v)


def _gla_log_forget(g_lr, w_a2, b_a2):
    z = (g_lr @ w_a2 + b_a2).astype(jnp.float32)
    return jax.nn.log_sigmoid(z) / GLA_TAU


def _gla_chunk(q, k, v, lg, s0):
    q, k, v = (t.astype(jnp.float32) for t in (q, k, v))
    L = q.shape[2]
    b = jnp.cumsum(lg, axis=2)
    causal = jnp.tril(jnp.ones((L, L), dtype=bool))
    diff = b[:, :, :, None, :] - b[:, :, None, :, :]
    decay = jnp.exp(jnp.where(causal[None, None, :, :, None], diff, -jnp.inf))
    a = jnp.einsum('bhid,bhjd,bhijd->bhij', q, k, decay)
    o = (jnp.einsum('bhij,bhje->bhie', a, v)
         + jnp.einsum('bhid,bhde->bhie', q * jnp.exp(b), s0))
    b_last = b[:, :, -1, :]
    s = (jnp.exp(b_last)[..., None] * s0
         + jnp.einsum('bhjd,bhje->bhde', k * jnp.exp(b_last[:, :, None, :] - b), v))
    return o, s


def _conv_ffn(h, hist, p):
    L = h.shape[1]
    a, g = jnp.split(h @ p['w_up'], 2, axis=-1)
    a_ext = jnp.concatenate([hist.astype(a.dtype), a], axis=1)
    conv = p['b_conv']
    for j in range(CONV_W):
        conv = conv + p['w_conv'][j] * a_ext[:, j:j + L]
    y = (jax.nn.gelu(conv) * g) @ p['w_down']
    return y, a_ext[:, L:]


def _mixer_inputs(x, c, p):
    B, L, _ = x.shape
    mods = _adaln(c, p['w_ada'], p['b_ada'])
    h = _modulate(x, p['g_norm1'], mods[0], mods[1])
    q_lat, kv_lat, kpe, gq, gk, gv, g_lr, og = _split_cols(h @ p['w_in'])
    ckv = _rms(kv_lat, p['g_kva'])
    gla = (gq.reshape(B, L, GLA_HEADS, GLA_DK) * (GLA_DK ** -0.5),
           gk.reshape(B, L, GLA_HEADS, GLA_DK),
           gv.reshape(B, L, GLA_HEADS, GLA_DV),
           _gla_log_forget(g_lr, p['w_a2'], p['b_a2']).reshape(B, L, GLA_HEADS, GLA_DK))
    return mods, q_lat, ckv, kpe, gla, og


def _layer_out(x, o_mla, o_gla, og, mods, conv_hist, p):
    B, L, _ = x.shape
    o_gla = _rms(o_gla.astype(x.dtype), p['g_gla']) * jax.nn.silu(og.reshape(B, L, GLA_HEADS, GLA_DV))
    mixed = jnp.concatenate([o_mla.reshape(B, L, MLA_HEADS * MLA_V),
                             o_gla.reshape(B, L, GLA_HEADS * GLA_DV)], axis=-1) @ p['w_out']
    x = x + mods[2][:, None, :] * mixed
    h = _modulate(x, p['g_norm2'], mods[3], mods[4])
    f, new_hist = _conv_ffn(h, conv_hist, p)
    return x + mods[5][:, None, :] * f, new_hist


def _layer_prompt(x, c, p):
    B, L, _ = x.shape
    pos = jnp.arange(L, dtype=jnp.int32)
    mods, q_lat, ckv, kpe, (gq, gk, gv, lg), og = _mixer_inputs(x, c, p)
    q = _mla_queries(q_lat, pos, p['g_qa'], p['w_uq'], p['g_qn'])
    k, v = _mla_keys(ckv, kpe, pos, p['w_ukv'], p['g_kn'])
    nb = L // Q_BLOCK
    q_blocks = q.reshape(B, nb, Q_BLOCK, MLA_HEADS, MLA_QK).swapaxes(0, 1)
    pos_blocks = pos.reshape(nb, Q_BLOCK)
    o_mla = lax.map(lambda blk: _attend(blk[0], k, v, blk[1], pos), (q_blocks, pos_blocks))
    o_mla = o_mla.swapaxes(0, 1).reshape(B, L, MLA_HEADS, MLA_V)
    nc = L // CHUNK
    to_blocks = lambda t: t.reshape(B, nc, CHUNK, GLA_HEADS, t.shape[-1]).transpose(1, 0, 3, 2, 4)
    s0 = jnp.zeros((B, GLA_HEADS, GLA_DK, GLA_DV), jnp.float32)

    def step(s, blk):
        o_b, s_new = _gla_chunk(blk[0], blk[1], blk[2], blk[3], s)
        return s_new, o_b

    s_fin, o_gla = lax.scan(step, s0, (to_blocks(gq), to_blocks(gk), to_blocks(gv), to_blocks(lg)))
    o_gla = o_gla.transpose(1, 0, 3, 2, 4).reshape(B, L, GLA_HEADS, GLA_DV)
    hist0 = jnp.zeros((B, CONV_W - 1, FFN_DIM), x.dtype)
    y, new_hist = _layer_out(x, o_mla, o_gla, og, mods, hist0, p)
    return y, ckv, kpe, s_fin, new_hist


def _layer_sample(x, c, cache_ckv, cache_kpe, s_gla, conv_hist, p):
    B, T, _ = x.shape
    P = cache_ckv.shape[1]
    q_pos = P + jnp.arange(T, dtype=jnp.int32)
    k_pos = jnp.arange(P + T, dtype=jnp.int32)
    mods, q_lat, ckv, kpe, (gq, gk, gv, lg), og = _mixer_inputs(x, c, p)
    q = _mla_queries(q_lat, q_pos, p['g_qa'], p['w_uq'], p['g_qn'])
    ckv_all = jnp.concatenate([cache_ckv.astype(ckv.dtype), ckv], axis=1)
    kpe_all = jnp.concatenate([cache_kpe.astype(kpe.dtype), kpe], axis=1)
    k, v = _mla_keys(ckv_all, kpe_all, k_pos, p['w_ukv'], p['g_kn'])
    o_mla = _attend(q, k, v, q_pos, k_pos)
    hd = lambda t: t.transpose(0, 2, 1, 3)
    o_gla, s_new = _gla_chunk(hd(gq), hd(gk), hd(gv), hd(lg), s_gla.astype(jnp.float32))
    o_gla = o_gla.transpose(0, 2, 1, 3)
    y, new_hist = _layer_out(x, o_mla, o_gla, og, mods, conv_hist, p)
    return y, ckv, kpe, s_new, new_hist


def setup_inputs(seed: int = 0) -> dict:
    key = jax.random.key(seed)
    ks = list(jax.random.split(key, 32))

    def nrm(i, shape, s=1.0):
        return jax.random.normal(ks[i], shape, jnp.float32) * s

    def gain(i, n):
        return 1.0 + nrm(i, (DEPTH, n), 0.02)

    return {
        'x_prompt': nrm(0, (BATCH, SEQ, D_MODEL)),
        'x_sample': nrm(1, (DEC_BATCH, DEC_SEQ, D_MODEL)),
        'c_prompt': nrm(2, (BATCH, D_MODEL)),
        'c_sample': nrm(3, (DEC_BATCH, D_MODEL)),
        'cache_ckv': nrm(4, (DEPTH, DEC_BATCH, PAST_LEN, MLA_KV_RANK)),
        'cache_kpe': nrm(5, (DEPTH, DEC_BATCH, PAST_LEN, MLA_ROPE)),
        'state_gla': nrm(6, (DEPTH, DEC_BATCH, GLA_HEADS, GLA_DK, GLA_DV)),
        'state_ffn_conv': nrm(7, (DEPTH, DEC_BATCH, CONV_W - 1, FFN_DIM)),
        'w_ada': nrm(8, (DEPTH, D_MODEL, 6 * D_MODEL), 0.5 * D_MODEL ** -0.5),
        'b_ada': nrm(9, (DEPTH, 6 * D_MODEL), 0.02),
        'g_norm1': gain(10, D_MODEL),
        'w_in': nrm(11, (DEPTH, D_MODEL, IN_COLS), D_MODEL ** -0.5),
        'g_qa': gain(12, MLA_Q_RANK),
        'w_uq': nrm(13, (DEPTH, MLA_Q_RANK, MLA_HEADS * MLA_QK), MLA_Q_RANK ** -0.5),
        'g_qn': gain(14, MLA_QK),
        'g_kva': gain(15, MLA_KV_RANK),
        'w_ukv': nrm(16, (DEPTH, MLA_KV_RANK, MLA_HEADS * (MLA_NOPE + MLA_V)), MLA_KV_RANK ** -0.5),
        'g_kn': gain(17, MLA_QK),
        'w_a2': nrm(18, (DEPTH, GLA_GATE_RANK, GLA_HEADS * GLA_DK), GLA_GATE_RANK ** -0.5),
        'b_a2': nrm(19, (DEPTH, GLA_HEADS * GLA_DK), 0.1),
        'g_gla': gain(20, GLA_DV),
        'w_out': nrm(21, (DEPTH, MIX_WIDTH, D_MODEL), MIX_WIDTH ** -0.5),
        'g_norm2': gain(22, D_MODEL),
        'w_up': nrm(23, (DEPTH, D_MODEL, 2 * FFN_DIM), D_MODEL ** -0.5),
        'w_conv': nrm(24, (DEPTH, CONV_W, FFN_DIM), CONV_W ** -0.5),
        'b_conv': nrm(25, (DEPTH, FFN_DIM), 0.02),
        'w_down': nrm(26, (DEPTH, FFN_DIM, D_MODEL), FFN_DIM ** -0.5),
    }


def reference(x_prompt, x_sample, c_prompt, c_sample, cache_ckv, cache_kpe, state_gla, state_ffn_conv,
              w_ada, b_ada, g_norm1, w_in, g_qa, w_uq, g_qn, g_kva, w_ukv, g_kn, w_a2, b_a2, g_gla,
              w_out, g_norm2, w_up, w_conv, b_conv, w_down):
    yp, ys = x_prompt, x_sample
    ckv_p, kpe_p, gla_p, conv_p = [], [], [], []
    ckv_s, kpe_s, gla_s, conv_s = [], [], [], []
    for l in range(DEPTH):
        p = {'w_ada': w_ada[l], 'b_ada': b_ada[l], 'g_norm1': g_norm1[l], 'w_in': w_in[l],
             'g_qa': g_qa[l], 'w_uq': w_uq[l], 'g_qn': g_qn[l], 'g_kva': g_kva[l],
             'w_ukv': w_ukv[l], 'g_kn': g_kn[l], 'w_a2': w_a2[l], 'b_a2': b_a2[l],
             'g_gla': g_gla[l], 'w_out': w_out[l], 'g_norm2': g_norm2[l], 'w_up': w_up[l],
             'w_conv': w_conv[l], 'b_conv': b_conv[l], 'w_down': w_down[l]}
        yp, a, b, s, h = _layer_prompt(yp, c_prompt, p)
        ckv_p.append(a); kpe_p.append(b); gla_p.append(s); conv_p.append(h)
        ys, a, b, s, h = _layer_sample(ys, c_sample, cache_ckv[l], cache_kpe[l], state_gla[l],
                                       state_ffn_conv[l], p)
        ckv_s.append(a); kpe_s.append(b); gla_s.append(s); conv_s.append(h)
    return (yp, ys, jnp.stack(ckv_p), jnp.stack(kpe_p), jnp.stack(gla_p), jnp.stack(conv_p),
            jnp.stack(ckv_s), jnp.stack(kpe_s), jnp.stack(gla_s), jnp.stack(conv_s))
```

```python
import contextlib
import math
import numpy as np
import concourse.bass as bass
import concourse.mybir as mybir
from concourse.bass_utils import run_bass_kernel_spmd

F32, BF16 = mybir.dt.float32, mybir.dt.bfloat16
ALU = mybir.AluOpType
AF = mybir.ActivationFunctionType
AX = mybir.AxisListType

D = 1024
FF = 2816
NJ = FF // 128
EPS = 1e-6
PAST = 1024
TS = 64


class Op:
    __slots__ = ("eng", "fn", "deps", "idx", "sig", "sem", "val", "dma", "ndma", "emitted")


class Prog:
    ENGS = ("pe", "act", "dve", "pool", "sp")

    def __init__(self):
        self.ops = []
        self.lw = {}
        self.rd = {}
        self.cnt = {e: 0 for e in self.ENGS}
        self.dcnt = {}
        self.waited = {e: {} for e in self.ENGS}
        self.sems = {}
        self.start = 0
        self.prev_final = {}

    def add(self, eng, fn, r=(), w=(), dma=None, ndma=1):
        import os
        if len(self.ops) >= int(os.environ.get('KMAXOPS', '100000000')):
            return None
        o = Op()
        o.eng, o.fn, o.dma, o.ndma = eng, fn, dma, ndma
        o.sig = False
        o.sem = None
        o.val = 0
        o.idx = len(self.ops)
        o.emitted = False
        deps = set()
        for k in r:
            p = self.lw.get(k)
            if p is not None:
                deps.add(p)
        for k in w:
            p = self.lw.get(k)
            if p is not None:
                deps.add(p)
            for q in self.rd.get(k, ()):
                deps.add(q)
        for k in r:
            self.rd.setdefault(k, []).append(o)
        for k in w:
            self.lw[k] = o
            self.rd[k] = []
        deps.discard(o)
        o.deps = deps
        self.ops.append(o)
        return o

    def emit(self, nc, semget, final=False):
        ops = self.ops[self.start:]
        for o in ops:
            for d in o.deps:
                if d.dma is None and not (d.eng == "pe" and o.eng == "pe"):
                    if d.emitted and not d.sig:
                        raise RuntimeError("dependency on already-emitted unsignalled op")
                    d.sig = True
        live = set(self.lw.values())
        for lst in self.rd.values():
            live.update(lst)
        for o in ops:
            if o.dma is None and o in live:
                o.sig = True
        for o in ops:
            if o.dma is not None:
                self.dcnt[o.dma] = self.dcnt.get(o.dma, 0) + 16 * o.ndma
                o.sem = ("d", o.dma)
                o.val = self.dcnt[o.dma]
            elif o.sig:
                self.cnt[o.eng] += 1
                o.sem = ("e", o.eng)
                o.val = self.cnt[o.eng]
        for sk in set(o.sem for o in ops if o.sem is not None):
            if sk not in self.sems:
                self.sems[sk] = semget(sk)

        prev_final = dict(self.prev_final)

        def body(ename):
            def run(e):
                waited = self.waited[ename]
                for sk, v in prev_final.items():
                    if waited.get(sk, 0) < v:
                        e.wait_ge(self.sems[sk], v)
                        waited[sk] = v
                for o in ops:
                    if o.eng != ename:
                        continue
                    need = {}
                    for d in o.deps:
                        if d.dma is None and d.eng == "pe" and ename == "pe":
                            continue
                        if d.sem is None:
                            raise RuntimeError("dep without sem")
                        if need.get(d.sem, 0) < d.val:
                            need[d.sem] = d.val
                    for sk, v in need.items():
                        if waited.get(sk, 0) < v:
                            e.wait_ge(self.sems[sk], v)
                            waited[sk] = v
                    if o.dma is not None:
                        o.fn(e, self.sems[o.sem])
                    else:
                        ins = o.fn(e)
                        if o.sig:
                            ins.then_inc(self.sems[o.sem], 1)
                    o.emitted = True
                if final and ename == "sp":
                    for sk, v in self.dcnt.items():
                        e.wait_ge(self.sems[("d", sk)], v)
                    for en in self.ENGS:
                        if self.cnt[en] > 0 and ("e", en) in self.sems:
                            e.wait_ge(self.sems[("e", en)], self.cnt[en])
            return run

        with nc.Block() as block:
            block.sync(body("sp"))
            block.tensor(body("pe"))
            block.scalar(body("act"))
            block.vector(body("dve"))
            block.gpsimd(body("pool"))
        self.start = len(self.ops)
        self.prev_final = {}
        for sk, v in self.dcnt.items():
            self.prev_final[("d", sk)] = v
        for en in self.ENGS:
            if self.cnt[en] > 0 and ("e", en) in self.sems:
                self.prev_final[("e", en)] = self.cnt[en]


class Ring:
    def __init__(self, tensors, name):
        self.t = tensors
        self.name = name
        self.i = 0

    def next(self):
        k = self.i % len(self.t)
        self.i += 1
        return self.t[k], (self.name, k)


class PsumAlloc:
    CLOCK = [lambda: 0]

    def __init__(self, banks, base=0):
        self.banks = banks
        self.base = base
        self.held = set()
        self.last = {k: -1 - (len(banks) - k) for k in range(len(banks))}

    def get(self):
        free = [k for k in range(len(self.banks)) if k not in self.held]
        if not free:
            raise RuntimeError("out of PSUM banks")
        k = min(free, key=lambda j: self.last[j])
        self.held.add(k)
        return self.banks[k], ("ps", self.base + k), k

    def rel(self, k):
        self.held.discard(k)
        self.last[k] = PsumAlloc.CLOCK[0]()


def bf(ap):
    return ap.bitcast(BF16)


import os as _os
LAG = int(_os.environ.get('KLAG', '5'))
SUBRR = int(_os.environ.get('KSUBRR', '1'))
XQ = _os.environ.get('KXQ', 'pool')
BTH = int(_os.environ.get('KBTH', '99'))


def build(TP, STAGE=9):
    nc = bass.Bass("TRN2", target_bir_lowering=False)
    P = Prog()
    PsumAlloc.CLOCK[0] = lambda: len(P.ops)
    stk = contextlib.ExitStack()

    def din(name, shape, dt=F32):
        return nc.dram_tensor(name, list(shape), dt, kind="ExternalInput").ap()

    def dout(name, shape, dt=F32):
        return nc.dram_tensor(name, list(shape), dt, kind="ExternalOutput").ap()

    def dscr(name, shape, dt=BF16):
        return nc.dram_tensor(name, list(shape), dt, kind="Internal").ap()

    I = {}
    I["xp"] = din("xp", [TP, D])
    I["xs"] = din("xs", [TS, D])
    I["cT"] = din("cT", [128, 8, 2])
    I["cckv"] = din("cckv", [PAST, 256])
    I["ckpe"] = din("ckpe", [PAST, 32])
    I["sgla"] = din("sgla", [4, 64, 128])
    I["sconv"] = din("sconv", [128, NJ, 2])
    I["wada"] = din("wada", [D, 6 * D])
    I["bada_fm"] = din("bada_fm", [128, 48])
    I["bada2"] = din("bada2", [2, 2048])
    I["g1_fm"] = din("g1_fm", [128, 8])
    I["g2_fm"] = din("g2_fm", [128, 8])
    I["win"] = din("win", [D, 2224])
    I["gqa_fm"] = din("gqa_fm", [128, 3])
    I["wuq"] = din("wuq", [384, 768])
    I["gqn8"] = din("gqn8", [1, 768])
    I["gkva"] = din("gkva", [1, 256])
    I["wukv"] = din("wukv", [256, 1024])
    I["gkn"] = din("gkn", [1, 96])
    I["wa2"] = din("wa2", [16, 256])
    I["ba2"] = din("ba2", [1, 256])
    I["ggla_fm"] = din("ggla_fm", [128, 1])
    I["wout"] = din("wout", [D, D])
    I["wup"] = din("wup", [D, 2 * FF])
    I["wconv_fm"] = din("wconv_fm", [128, NJ, 3])
    I["bconv_fm"] = din("bconv_fm", [128, NJ])
    I["wdown"] = din("wdown", [FF, D])
    I["ident"] = din("ident", [128, 128])
    I["tri"] = din("tri", [128, 128])
    I["ones"] = din("ones", [128, 128])
    I["tab"] = din("tab", [max(TP, PAST + TS), 64])
    I["sel2"] = din("sel2", [2, 2, 128])
    O = {}
    O["yp"] = dout("yp", [TP, D])
    O["ys"] = dout("ys", [TS, D])
    O["ckvp"] = dout("ckvp", [TP, 256])
    O["kpep"] = dout("kpep", [TP, 32])
    O["glap"] = dout("glap", [4, 64, 128])
    O["convp"] = dout("convp", [2, FF])
    O["ckvs"] = dout("ckvs", [TS, 256])
    O["kpes"] = dout("kpes", [TS, 32])
    O["glas"] = dout("glas", [4, 64, 128])
    O["convs"] = dout("convs", [2, FF])

    def mkseq(name, T, past, s, x, y, ckvo, kpeo, glao, convo):
        TK = past + T
        ntk = (TK + 127) // 128
        return dict(name=name, T=T, past=past, s=s, x=x, y=y, ckvo=ckvo, kpeo=kpeo, glao=glao, convo=convo,
                    TK=TK, ntk=ntk,
                    kTs=dscr("kTs_" + name, [8, 96, ntk * 128]),
                    vs=dscr("vs_" + name, [4, 128, ntk, 192]),
                    mxs=dscr("mxs_" + name, [D, T]))

    seqs = [mkseq("s", TS, PAST, 1, I["xs"], O["ys"], O["ckvs"], O["kpes"], O["glas"], O["convs"]),
            mkseq("p", TP, 0, 0, I["xp"], O["yp"], O["ckvp"], O["kpep"], O["glap"], O["convp"])]
    wups = dscr("wups", [2 * NJ, 128, 8, 128])
    wdowns = dscr("wdowns", [FF, D])
    gbcs = dscr("gbcs", [2, 128, 2048], F32)

    semstack = contextlib.ExitStack()
    semn = [0]

    def semget(sk):
        semn[0] += 1
        return semstack.enter_context(nc.semaphore("s%d" % semn[0]))

    def sb(st, name, shape, dt=F32):
        return st.enter_context(nc.sbuf_tensor("S_" + name, list(shape), dt))

    def ring(st, name, shape, dt, n):
        return Ring([sb(st, "%s%d" % (name, i), shape, dt) for i in range(n)], name)

    def dma(eng, out, in_, key, r=(), w=()):
        def fn(e, sem):
            e.dma_start(out=out, in_=in_).then_inc(sem, 16)
        return P.add(eng, fn, r=r, w=w, dma=key)

    top = contextlib.ExitStack()
    with semstack, top:
        banks = [top.enter_context(nc.psum_tensor("psb%d" % i, [128, 512], F32)) for i in range(8)]
        PS = PsumAlloc(banks)
        identf = sb(top, "identf", [128, 128])
        identb = sb(top, "identb", [128, 128], BF16)
        trif = sb(top, "trif", [128, 128])
        onesf = sb(top, "onesf", [128, 128])
        mods = sb(top, "mods", [128, 48, 2])
        mul1 = sb(top, "mul1", [128, 2, 8])
        mul2 = sb(top, "mul2", [128, 2, 8])
        g1fm = sb(top, "g1fm", [128, 8])
        g2fm = sb(top, "g2fm", [128, 8])
        dma("sp", identf[:], I["ident"], "c_identf", w=["identf"])
        dma("sp", trif[:], I["tri"], "c_trif", w=["trif"])
        dma("sp", onesf[:], I["ones"], "c_onesf", w=["onesf"])
        dma("sp", g1fm[:], I["g1_fm"], "c_g1", w=["g1fm"])
        dma("sp", g2fm[:], I["g2_fm"], "c_g2", w=["g2fm"])
        P.add("dve", lambda e: e.tensor_copy(out=identb[:], in_=identf[:]), r=["identf"], w=["identb"])

        with contextlib.ExitStack() as st:
            cTf = sb(st, "cTf", [128, 8, 2])
            scT = sb(st, "scT", [128, 8, 2], BF16)
            acc = sb(st, "acc", [128, 96])
            badaf = sb(st, "badaf", [128, 48])
            bada2 = sb(st, "bada2", [2, 2048])
            gates = sb(st, "gates", [2, 2048])
            sel2 = sb(st, "sel2", [2, 2, 128])
            gst = sb(st, "gst", [128, 2048])
            war = ring(st, "wa", [128, 6 * D], BF16, 2)
            for c in range(8):
                dma("pool", wups.rearrange("j p c n -> p c j n")[:, c, :, :],
                    I["wup"][c * 128:(c + 1) * 128, :].rearrange("p (j n) -> p j n", n=128),
                    "wups%d" % c, w=[("wups", c)])
            dma("pool", wdowns, I["wdown"], "wdowns", w=["wdowns"])
            dma("sp", cTf[:], I["cT"], "c_cT", w=["cTf"])
            dma("sp", badaf[:], I["bada_fm"], "c_badaf", w=["badaf"])
            dma("sp", bada2[:], I["bada2"], "c_bada2", w=["bada2"])
            dma("sp", sel2[:], I["sel2"].rearrange("s k m -> k s m"), "c_sel2", w=["sel2"])
            P.add("act", lambda e: e.activation(out=scT[:], in_=cTf[:], func=AF.Silu), r=["cTf"], w=["scT"])
            P.add("dve", lambda e: e.memset(acc[:], 0.0), w=["acc"])
            gb = [PS.get() for _ in range(4)]
            for c in range(8):
                wa, kwa = war.next()
                dma("pool", wa[:], I["wada"][c * 128:(c + 1) * 128, :], "wa%d" % (c % 2), w=[kwa])
                pb, kb, ib = PS.get()

                def fm(e, wa=wa, pb=pb, c=c):
                    for m in range(48):
                        ins = e.matmul(pb[:, 2 * m:2 * m + 2], lhsT=wa[:, m * 128:(m + 1) * 128], rhs=scT[:, c, :],
                                       start=True, stop=True)
                    return ins
                P.add("pe", fm, r=[kwa, "scT"], w=[kb])
                P.add("dve", lambda e, pb=pb: e.tensor_tensor(out=acc[:], in0=pb[:, 0:96], in1=acc[:], op=ALU.add),
                      r=[], w=[kb, "acc"])
                PS.rel(ib)
                for gi in range(4):
                    col = (2048 if gi < 2 else 5120) + (gi % 2) * 512
                    P.add("pe", lambda e, wa=wa, c=c, gi=gi, col=col: e.matmul(
                        gb[gi][0][0:2, :], lhsT=scT[:, c, :], rhs=wa[:, col:col + 512], start=(c == 0), stop=(c == 7)),
                        r=[kwa, "scT"], w=[gb[gi][1]])
            P.add("dve", lambda e: e.tensor_tensor(
                out=mods[:], in0=acc[:].rearrange("p (m s) -> p m s", s=2),
                in1=badaf[:].unsqueeze(2).broadcast_to([128, 48, 2]), op=ALU.add), r=["acc", "badaf"], w=["mods"])
            for s in range(2):
                for (dst, gfm, sc) in ((mul1, g1fm, 1), (mul2, g2fm, 4)):
                    P.add("dve", lambda e, dst=dst, gfm=gfm, sc=sc, s=s: e.scalar_tensor_tensor(
                        out=dst[:, s, :], in0=mods[:, sc * 8:(sc + 1) * 8, s], scalar=1.0, in1=gfm[:],
                        op0=ALU.add, op1=ALU.mult), r=["mods", "g1fm", "g2fm"], w=[("mul", id(dst), s)])
            for gi in range(4):
                P.add("dve", lambda e, gi=gi: e.tensor_tensor(
                    out=gates[:, gi * 512:(gi + 1) * 512], in0=gb[gi][0][0:2, :], in1=bada2[:, gi * 512:(gi + 1) * 512],
                    op=ALU.add), r=["bada2"], w=[gb[gi][1], "gates"])
                PS.rel(gb[gi][2])
            for s in range(2):
                for gi in range(4):
                    pb, kb, ib = PS.get()
                    P.add("pe", lambda e, pb=pb, s=s, gi=gi: e.matmul(
                        pb[:, :], lhsT=sel2[:, s, :], rhs=gates[:, gi * 512:(gi + 1) * 512], start=True, stop=True),
                        r=["sel2", "gates"], w=[kb])
                    P.add("act", lambda e, pb=pb, gi=gi: e.activation(
                        out=gst[:, gi * 512:(gi + 1) * 512], in_=pb[:, :], func=AF.Copy), w=[kb, "gst"])
                    PS.rel(ib)
                dma("sp", gbcs[s], gst[:], "gbcs_w", r=["gst"], w=[("gbcs", s)])
            P.emit(nc, semget)

        with contextlib.ExitStack() as st:
            PS_full = PsumAlloc(banks, 0)
            PS_a5 = PsumAlloc(banks[0:5], 0)
            PS_b3 = PsumAlloc(banks[5:8], 5)
            CFG = {"A": PS_full, "B": PS_full, "LA": 5}

            def set_mode(split):
                if split:
                    CFG.update(A=PS_a5, B=PS_b3, LA=2)
                else:
                    CFG.update(A=PS_full, B=PS_full, LA=5)
            winb = sb(st, "winb", [128, 8, 2224], BF16)
            wuqb = sb(st, "wuqb", [128, 3, 768], BF16)
            wukvb = sb(st, "wukvb", [128, 2, 1024], BF16)
            wa2b = sb(st, "wa2b", [16, 256], BF16)
            gqafm = sb(st, "gqafm", [128, 3])
            gqnbc = sb(st, "gqnbc", [128, 768])
            gkvabc = sb(st, "gkvabc", [128, 256])
            gknbc = sb(st, "gknbc", [128, 96])
            ba2bc = sb(st, "ba2bc", [128, 256])
            dma("pool", winb[:], I["win"].rearrange("(c p) n -> p c n", p=128), "l_win", w=["winb"])
            dma("pool", wukvb[:], I["wukv"].rearrange("(c p) n -> p c n", p=128), "l_wukv", w=["wukvb"])
            dma("pool", wa2b[:], I["wa2"], "l_wa2", w=["wa2b"])
            dma("sp", gqafm[:], I["gqa_fm"], "l_gqa", w=["gqafm"])
            dma("sp", gqnbc[:], I["gqn8"].partition_broadcast(128), "l_gqn", w=["gqnbc"])
            dma("sp", gkvabc[:], I["gkva"].partition_broadcast(128), "l_gkva", w=["gkvabc"])
            dma("sp", gknbc[:], I["gkn"].partition_broadcast(128), "l_gkn", w=["gknbc"])
            dma("sp", ba2bc[:], I["ba2"].partition_broadcast(128), "l_ba2", w=["ba2bc"])
            P.add("dve", lambda e: e.tensor_scalar(out=gqnbc[:], in0=gqnbc[:], scalar1=96.0 ** -0.5, scalar2=None,
                                                   op0=ALU.mult), r=[], w=["gqnbc"])

            xr = ring(st, "xt", [128, D], F32, 2)
            for c in range(3):
                xt_, kx_ = xr.next()
                dma("sp", xt_[:, 0:768], I["wuq"][c * 128:(c + 1) * 128, :], "l_wuq%d" % kx_[1], w=[kx_])
                P.add("dve", lambda e, c=c, xt_=xt_: e.tensor_scalar(out=wuqb[:, c, :], in0=xt_[:, 0:768], scalar1=gqafm[:, c:c + 1],
                                                                    scalar2=None, op0=ALU.mult), r=[kx_, "gqafm"], w=["wuqb"])
            junk = sb(st, "junk", [128, D], BF16)
            xsr = ring(st, "xs_", [128, D], BF16, 2)
            hTr = ring(st, "hT", [128, 8, 128], BF16, 2)
            st8 = ring(st, "st8", [128, 64], F32, 4)
            sqf = sb(st, "sqf", [128, 768])
            qn = sb(st, "qn", [128, 768])
            qln = sb(st, "qln", [128, 384], BF16)
            qlT = sb(st, "qlT", [128, 3, 128], BF16)
            qb = sb(st, "qb", [128, 768], BF16)
            tmpa = sb(st, "tmpa", [128, 256])
            tmpb = sb(st, "tmpb", [128, 256])
            ckvr = ring(st, "ckvf", [128, 256], F32, 2)
            kper = ring(st, "kpef", [128, 32], F32, 2)
            glrb = sb(st, "glrb", [128, 16], BF16)
            ckvb = sb(st, "ckvb", [128, 256], BF16)
            ckvT = sb(st, "ckvT", [128, 2, 128], BF16)
            sqk = sb(st, "sqk", [128, 512])
            kn = sb(st, "kn", [128, 768])
            kb_ = sb(st, "kb_", [128, 768], BF16)
            kg = sb(st, "kg", [128, 32])
            kr = sb(st, "kr", [128, 32])
            vstr = ring(st, "vst", [128, 4, 192], BF16, 2)
            tabr = ring(st, "tab", [128, 64], F32, 2)
            glrT = sb(st, "glrT", [16, 128], BF16)
            zb = sb(st, "zb", [128, 256])
            ez = sb(st, "ez", [128, 256])
            spl = sb(st, "spl", [128, 256])
            eb = sb(st, "eb", [128, 256])
            enb = sb(st, "enb", [128, 256])
            gfm = sb(st, "gfm", [64, 4])
            qt = sb(st, "qt", [128, 256], BF16)
            kt = sb(st, "kt", [128, 256], BF16)
            vb = sb(st, "vb", [128, 512], BF16)
            qkT = sb(st, "qkT", [64, 8, 128], BF16)
            ATb = sb(st, "ATb", [128, 4, 128], BF16)
            sg = sb(st, "sg", [128, 512])
            tmpo = sb(st, "tmpo", [128, 512])
            gated = sb(st, "gated", [128, 512], BF16)
            Sf = sb(st, "Sf", [64, 4, 128])
            Sg_ = sb(st, "Sg_", [64, 4, 128])
            Sb = sb(st, "Sb", [64, 4, 128], BF16)
            QTr = ring(st, "QT", [96, 8, 512], BF16, 2)
            KTb = sb(st, "KTb", [96, 8, 512], BF16)
            mxr = ring(st, "mxb", [128, 8, 512], BF16, 2)
            wsr = ring(st, "wsb", [128, 2224], F32, 2)
            vchr = ring(st, "vch", [128, 16, 192], BF16, 2)
            kchr = ring(st, "kch", [96, 2048], BF16, 2)
            ptr = ring(st, "pt", [128, 512], BF16, 6)
            recr = ring(st, "rec", [128, 512], F32, 1)
            osbr = ring(st, "osb", [128, 512], F32, 2)
            for t_ in vstr.t:
                P.add("dve", lambda e, t_=t_: e.memset(t_[:, :, :], 0.0), w=[("vst", vstr.t.index(t_))])
                P.add("dve", lambda e, t_=t_: e.memset(t_[:, :, 64:128], 1.0), w=[("vst", vstr.t.index(t_))])

            def rstd_chain(src, n, scale, key, tn):
                P.add("act", lambda e: e.activation(out=src[0:tn, 0:n], in_=src[0:tn, 0:n], func=AF.Ln, scale=scale, bias=EPS),
                      r=[], w=[key])
                P.add("act", lambda e: e.activation(out=src[0:tn, 0:n], in_=src[0:tn, 0:n], func=AF.Exp, scale=-0.5),
                      r=[], w=[key])

            def transposes(srcs, tn, key_r, evac, key_w, extra_w=()):
                pb, kb, ib = CFG["B"].get()
                pbv = bf(pb[:, :])

                def fn(e):
                    for i, sap in enumerate(srcs):
                        m = sap.shape[1]
                        ins = e.transpose(out=pbv[0:m, i * tn:(i + 1) * tn], in_=sap, identity=identb[0:tn, 0:tn])
                    return ins
                P.add("pe", fn, r=list(key_r) + ["identb"], w=[kb])
                evac(pbv, kb)
                CFG["B"].rel(ib)

            def keyside(sq, ti, tn, ckvf, kckv, kpef, kkpe, tab, ktab, col):
                s8, ks8 = st8.next()
                vst, kvst = vstr.next()
                P.add("act", lambda e: e.activation(out=ckvb[0:tn, :], in_=ckvf[0:tn, :], func=AF.Copy), r=[kckv], w=["ckvb"])

                def ev(pbv, kb):
                    P.add("dve", lambda e: e.tensor_copy(out=ckvT[:, :, 0:tn], in_=pbv[:, 0:2 * tn].rearrange("p (c t) -> p c t", c=2)),
                          r=[], w=[kb, "ckvT"])
                transposes([ckvb[0:tn, c * 128:(c + 1) * 128] for c in range(2)], tn, ["ckvb"], ev, "ckvT")
                yield
                kbk = [CFG["B"].get() for _ in range(2)]
                for g in range(2):
                    def mm(e, g=g):
                        for c in range(2):
                            ins = e.matmul(kbk[g][0][0:tn, :], lhsT=ckvT[:, c, 0:tn], rhs=wukvb[:, c, g * 512:(g + 1) * 512],
                                           start=(c == 0), stop=(c == 1))
                        return ins
                    P.add("pe", mm, r=["ckvT", "wukvb"], w=[kbk[g][1]])
                for g in range(2):
                    K3 = kbk[g][0][0:tn, :].rearrange("p (h d) -> p h d", h=4)
                    P.add("act", lambda e, K3=K3, g=g: e.activation(
                        out=sqk[0:tn, g * 256:(g + 1) * 256].rearrange("p (h d) -> p h d", h=4), in_=K3[:, :, 0:64], func=AF.Square),
                        w=[kbk[g][1], "sqk"])
                    K4 = kbk[g][0][0:tn, :].rearrange("p (a b d) -> p a b d", a=2, b=2)
                    for b2 in range(2):
                        P.add("act", lambda e, K4=K4, g=g, b2=b2: e.activation(
                            out=vst[0:tn, 2 * g:2 * g + 2, b2 * 128:b2 * 128 + 64], in_=K4[:, :, b2, 64:128], func=AF.Copy),
                            w=[kbk[g][1], kvst])
                yield
                P.add("dve", lambda e: e.tensor_reduce(out=s8[0:tn, 0:8], in_=sqk[0:tn, :].rearrange("p (h d) -> p h d", h=8),
                                                        axis=AX.X, op=ALU.add), r=["sqk"], w=[ks8])
                P.add("act", lambda e: e.activation(out=junk[0:tn, 0:32], in_=kpef[0:tn, :], func=AF.Square,
                                                    accum_out=s8[0:tn, 8:9]), r=[kkpe], w=["junk", ks8])
                P.add("dve", lambda e: e.tensor_scalar(out=s8[0:tn, 0:8], in0=s8[0:tn, 0:8], scalar1=s8[0:tn, 8:9], scalar2=None,
                                                       op0=ALU.add), w=[ks8])
                rstd_chain(s8, 8, 1.0 / 96, ks8, tn)
                for g in range(2):
                    K3 = kbk[g][0][0:tn, :].rearrange("p (h d) -> p h d", h=4)
                    P.add("dve", lambda e, K3=K3, g=g: e.tensor_tensor(
                        out=kn[0:tn, :].rearrange("p (h d) -> p h d", h=8)[:, 4 * g:4 * g + 4, 0:64], in0=K3[:, :, 0:64],
                        in1=s8[0:tn, 4 * g:4 * g + 4].unsqueeze(2).broadcast_to([tn, 4, 64]), op=ALU.mult),
                        r=[ks8], w=[kbk[g][1], "kn"])
                    CFG["B"].rel(kbk[g][2])
                yield
                kn3 = kn[0:tn, :].rearrange("p (h d) -> p h d", h=8)
                kb3 = kb_[0:tn, :].rearrange("p (h d) -> p h d", h=8)
                P.add("dve", lambda e: e.tensor_tensor(out=kb3[:, :, 0:64], in0=kn3[:, :, 0:64],
                                                       in1=gknbc[0:tn, 0:64].unsqueeze(1).broadcast_to([tn, 8, 64]), op=ALU.mult),
                      r=["kn", "gknbc"], w=["kb_"])
                P.add("dve", lambda e: e.tensor_tensor(out=kg[0:tn, :], in0=kpef[0:tn, :], in1=gknbc[0:tn, 64:96], op=ALU.mult),
                      r=[kkpe, "gknbc"], w=["kg"])
                P.add("dve", lambda e: e.tensor_tensor(out=kr[0:tn, :], in0=kg[0:tn, :], in1=tab[0:tn, 0:32], op=ALU.mult),
                      r=["kg", ktab], w=["kr"])
                P.add("dve", lambda e: e.tensor_tensor(out=tmpa[0:tn, 0:16], in0=kg[0:tn, 16:32], in1=tab[0:tn, 32:48], op=ALU.mult),
                      r=["kg", ktab], w=["tmpa"])
                P.add("dve", lambda e: e.tensor_tensor(out=tmpa[0:tn, 16:32], in0=kg[0:tn, 0:16], in1=tab[0:tn, 48:64], op=ALU.mult),
                      r=["kg", ktab], w=["tmpa"])
                P.add("dve", lambda e: e.tensor_tensor(out=kr[0:tn, :], in0=kr[0:tn, :], in1=tmpa[0:tn, 0:32], op=ALU.add),
                      r=["tmpa"], w=["kr"])
                P.add("dve", lambda e: e.tensor_tensor(out=kb3[:, :, 64:96], in0=kr[0:tn, :].unsqueeze(1).broadcast_to([tn, 8, 32]),
                                                       in1=s8[0:tn, 0:8].unsqueeze(2).broadcast_to([tn, 8, 32]), op=ALU.mult),
                      r=["kr", ks8], w=["kb_"])

                yield

                def ev2(pbv, kb):
                    P.add("dve", lambda e: e.tensor_copy(out=KTb[:, :, col:col + tn],
                                                         in_=pbv[0:96, 0:8 * tn].rearrange("p (h t) -> p h t", h=8)),
                          r=[], w=[kb, "KTb"])
                transposes([kb_[0:tn, h * 96:(h + 1) * 96] for h in range(8)], tn, ["kb_"], ev2, "KTb")
                dma("sp", sq["vs"][:, :, ti, :].rearrange("a p d -> p a d"), vst[:, :, :], "vsw%d" % kvst[1],
                    r=[kvst], w=[("vs", sq["name"], ti)])

            def attention(sq, q0, NQ, QT, kQT, mxb, kmx, bg=None, nyield=120):
                qa = sq["past"] + q0
                tlast = (qa + NQ - 1) // 128
                nopp = 8 * (tlast + 1) + 12
                tick = [0, 0]
                kpump = 1

                def pump(k):
                    if bg is None:
                        return
                    tick[0] += k
                    want = (tick[0] * nyield) // nopp
                    while tick[1] < want:
                        tick[1] += 1
                        try:
                            next(bg)
                        except StopIteration:
                            return
                def _pair(pp):
                    bo = [CFG["A"].get() for _ in range(2)]
                    items = []
                    for kc in range(0, tlast + 1, 16):
                        nt = min(16, tlast + 1 - kc)
                        for hh in range(2):
                            for tl in range(nt):
                                items.append(dict(kc=kc, nt=nt, hh=hh, tl=tl))
                    cur = {}

                    def issue_S(it):
                        kc, nt, hh, tl = it["kc"], it["nt"], it["hh"], it["tl"]
                        h = 2 * pp + hh
                        if hh == 0 and tl == 0:
                            vch, kvch = vchr.next()
                            dma("sp", vch[:, 0:nt, :], sq["vs"][pp, :, kc:kc + nt, :], "vch%d" % kvch[1],
                                r=[("vs", sq["name"], t) for t in range(kc, kc + nt)], w=[kvch])
                            cur["v"] = (vch, kvch)
                        if tl == 0:
                            kch, kkch = kchr.next()
                            ncol = min((kc + nt) * 128, sq["TK"]) - kc * 128
                            dma("sp", kch[:, 0:ncol], sq["kTs"][h, :, kc * 128:kc * 128 + ncol], "kch%d" % kkch[1],
                                r=[("kT", sq["name"], t) for t in range(kc, kc + nt)], w=[kkch])
                            cur["k"] = (kch, kkch)
                        kch, kkch = cur["k"]
                        it["v"] = cur["v"]
                        t = kc + tl
                        tk = min(128, sq["TK"] - 128 * t)
                        jlo = 128 * t - qa
                        qlo = max(0, jlo)
                        pb, kb, ib = CFG["A"].get()
                        it.update(t=t, tk=tk, jlo=jlo, qlo=qlo, pb=pb, kb=kb, ib=ib)
                        P.add("pe", lambda e: e.matmul(
                            pb[0:tk, qlo:NQ], lhsT=kch[:, tl * 128:tl * 128 + tk], rhs=QT[:, h, qlo:NQ], start=True, stop=True),
                            r=[kkch, kQT], w=[kb])

                    def issue_rest(it):
                        hh, tl, t, tk, jlo, qlo, pb, kb, ib = (it[k] for k in ("hh", "tl", "t", "tk", "jlo", "qlo", "pb", "kb", "ib"))
                        vch, kvch = it["v"]
                        pt, kpt = ptr.next()
                        P.add("act", lambda e: e.activation(out=pt[0:tk, qlo:NQ], in_=pb[0:tk, qlo:NQ], func=AF.Exp), w=[kb, kpt])
                        CFG["A"].rel(ib)
                        if jlo >= 0 and tk == 128:
                            P.add("dve", lambda e: e.memset(pt[64:128, qlo:qlo + 64], 0.0), w=[kpt])
                        P.add("pe", lambda e: e.matmul(
                            bo[hh][0][:, qlo:NQ], lhsT=vch[0:tk, tl, hh * 64:hh * 64 + 128], rhs=pt[0:tk, qlo:NQ],
                            start=(t == 0), stop=(t == tlast)), r=[kpt, kvch], w=[bo[hh][1]])
                    LA = CFG["LA"]
                    for i in range(len(items) + LA):
                        if i < len(items):
                            issue_S(items[i])
                        if i >= LA:
                            issue_rest(items[i - LA])
                        pump(kpump)
                    for hh in range(2):
                        rec, krec = recr.next()
                        osb, kosb = osbr.next()
                        so, oo = ((64, 0), (0, 64))[hh]
                        P.add("dve", lambda e, osb=osb, hh=hh: e.tensor_copy(out=osb[:, 0:NQ], in_=bo[hh][0][:, 0:NQ]), w=[bo[hh][1], kosb])
                        CFG["A"].rel(bo[hh][2])
                        P.add("dve", lambda e, rec=rec, osb=osb, so=so, oo=oo: e.reciprocal(out=rec[oo:oo + 64, 0:NQ], in_=osb[so:so + 64, 0:NQ]),
                              r=[kosb], w=[krec])
                        P.add("dve", lambda e, rec=rec, osb=osb, oo=oo, pp=pp: e.tensor_tensor(
                            out=mxb[oo:oo + 64, pp, 0:NQ], in0=osb[oo:oo + 64, 0:NQ], in1=rec[oo:oo + 64, 0:NQ], op=ALU.mult),
                            r=[krec, kosb], w=[kmx])

                for pp in range(4):
                    _pair(pp)
                pump(100000)

            def _p1seq(sq):
                s = sq["s"]
                T, past = sq["T"], sq["past"]
                if past:
                    dma("sp", Sf[:, :, :], I["sgla"].rearrange("h k v -> k h v"), "l_S", w=["Sf"])
                    P.add("act", lambda e: e.activation(out=Sb[:], in_=Sf[:], func=AF.Copy), r=["Sf"], w=["Sb"])
                else:
                    P.add("dve", lambda e: e.memset(Sf[:], 0.0), w=["Sf"])
                    P.add("dve", lambda e: e.memset(Sb[:], 0.0), w=["Sb"])
                def _ctile(ti):
                    col = (ti % 4) * 128
                    ckvf, kckv = ckvr.next()
                    kpef, kkpe = kper.next()
                    tab, ktab = tabr.next()
                    dma("sp", ckvf[:], I["cckv"][ti * 128:(ti + 1) * 128, :], "l_ckv%d" % kckv[1], w=[kckv])
                    dma("sp", kpef[:], I["ckpe"][ti * 128:(ti + 1) * 128, :], "l_kpe%d" % kkpe[1], w=[kkpe])
                    dma("sp", tab[:], I["tab"][ti * 128:(ti + 1) * 128, :], "l_tab%d" % ktab[1], w=[ktab])
                    yield from keyside(sq, ti, 128, ckvf, kckv, kpef, kkpe, tab, ktab, col)
                for t4 in range(0, past // 128, 4):
                    pend = list(range(t4, t4 + 4))
                    run = []
                    since = 10 ** 9
                    while pend or run:
                        if pend and len(run) < 2 and since >= 3:
                            run.append(_ctile(pend.pop(0)))
                            since = 0
                        for g_ in list(run):
                            try:
                                next(g_)
                            except StopIteration:
                                run.remove(g_)
                        since += 1
                    dma("sp", sq["kTs"][:, :, t4 * 128:(t4 + 4) * 128].rearrange("h d t -> d h t"), KTb[:, :, :], "kTw",
                        r=["KTb"], w=[("kT", sq["name"], t) for t in range(t4, t4 + 4)])
                NQB = min(512, T)
                ntile = (NQB + 127) // 128

                def _p1blk(q0, info, gen=False):
                    QT, kQT = QTr.next()
                    mxb, kmx = mxr.next()
                    info.update(QT=QT, kQT=kQT, mxb=mxb, kmx=kmx)

                    def _p1tile(it):
                        t0 = q0 + it * 128
                        tn = min(128, T - t0)
                        col = it * 128
                        ti = (past + t0) // 128
                        xt, kx = xr.next()
                        xs_, kxs = xsr.next()
                        hT, khT = hTr.next()
                        s8, ks8 = st8.next()
                        tab, ktab = tabr.next()
                        ckvf, kckv = ckvr.next()
                        kpef, kkpe = kper.next()
                        wsb, kws = wsr.next()
                        dma(XQ, xt[0:tn, :], sq["x"][t0:t0 + tn, :], "l_x%d" % kx[1], w=[kx])
                        dma(XQ, tab[0:tn, :], I["tab"][past + t0:past + t0 + tn, :], "l_tab%d" % ktab[1], w=[ktab])
                        P.add("act", lambda e, xt=xt, s8=s8, tn=tn: e.activation(out=junk[0:tn, :], in_=xt[0:tn, :], func=AF.Square,
                                                                                 accum_out=s8[0:tn, 16:17]), r=[kx], w=["junk", ks8])
                        rstd_chain(s8[:, 16:17], 1, 1.0 / D, ks8, tn)
                        P.add("act", lambda e, xt=xt, xs_=xs_, s8=s8, tn=tn: e.activation(
                            out=xs_[0:tn, :], in_=xt[0:tn, :], func=AF.Identity, scale=s8[0:tn, 16:17]), r=[kx, ks8], w=[kxs])

                        def ev_h(pbv, kb, hT=hT, khT=khT, tn=tn, s=s):
                            def fn(e):
                                for c in range(8):
                                    ins = e.tensor_scalar(out=hT[:, c, 0:tn], in0=pbv[:, c * tn:(c + 1) * tn], scalar1=mul1[:, s, c:c + 1],
                                                          scalar2=mods[:, c, s:s + 1], op0=ALU.mult, op1=ALU.add)
                                return ins
                            P.add("dve", fn, r=["mods", ("mul", id(mul1), s)], w=[kb, khT])
                        transposes([xs_[0:tn, c * 128:(c + 1) * 128] for c in range(8)], tn, [kxs], ev_h, khT)
                        yield
                        groups = [(0, 384), (384, 288), (672, 512), (1184, 512), (1712, 512)]
                        for gi, (c0, n) in enumerate(groups):
                            gb_, gk_, gi_ = CFG["B"].get()

                            def mm(e, gb_=gb_, c0=c0, n=n):
                                for c in range(8):
                                    ins = e.matmul(gb_[0:tn, 0:n], lhsT=hT[:, c, 0:tn], rhs=winb[:, c, c0:c0 + n],
                                                   start=(c == 0), stop=(c == 7))
                                return ins
                            P.add("pe", mm, r=[khT, "winb"], w=[gk_])
                            if gi == 1:
                                def mm2(e, gb_=gb_):
                                    for c in range(8):
                                        ins = e.matmul(gb_[0:tn, 288:304], lhsT=hT[:, c, 0:tn], rhs=winb[:, c, 1696:1712],
                                                       start=(c == 0), stop=(c == 7))
                                    return ins
                                P.add("pe", mm2, r=[khT, "winb"], w=[gk_])
                                P.add("dve", lambda e, gb_=gb_: e.tensor_copy(out=wsb[0:tn, 384:672], in_=gb_[0:tn, 0:288]), w=[gk_, kws])
                                P.add("dve", lambda e, gb_=gb_: e.tensor_copy(out=wsb[0:tn, 1696:1712], in_=gb_[0:tn, 288:304]), w=[gk_, kws])
                            elif gi == 0:
                                P.add("act", lambda e, gb_=gb_: e.activation(out=wsb[0:tn, 0:384], in_=gb_[0:tn, 0:384], func=AF.Copy), w=[gk_, kws])
                            elif gi == 2:
                                P.add("dve", lambda e, gb_=gb_: e.tensor_copy(out=wsb[0:tn, 672:1184], in_=gb_[0:tn, 0:512]), w=[gk_, kws])
                            elif gi == 3:
                                P.add("act", lambda e, gb_=gb_: e.activation(out=wsb[0:tn, 1184:1696], in_=gb_[0:tn, :], func=AF.Copy), w=[gk_, kws])
                            else:
                                P.add("dve", lambda e, gb_=gb_: e.tensor_copy(out=wsb[0:tn, 1712:2224], in_=gb_[0:tn, :]), w=[gk_, kws])
                            CFG["B"].rel(gi_)
                            yield
                        P.add("act", lambda e, tn=tn, s8=s8: e.activation(out=junk[0:tn, 0:256], in_=wsb[0:tn, 384:640], func=AF.Square,
                                                                          accum_out=s8[0:tn, 17:18]), r=[kws], w=["junk", ks8])
                        rstd_chain(s8[:, 17:18], 1, 1.0 / 256, ks8, tn)
                        P.add("dve", lambda e, tn=tn, s8=s8, ckvf=ckvf: e.scalar_tensor_tensor(
                            out=ckvf[0:tn, :], in0=wsb[0:tn, 384:640], scalar=s8[0:tn, 17:18], in1=gkvabc[0:tn, :],
                            op0=ALU.mult, op1=ALU.mult), r=[ks8, "gkvabc", kws], w=[kckv])
                        P.add("act", lambda e, tn=tn, kpef=kpef: e.activation(out=kpef[0:tn, :], in_=wsb[0:tn, 640:672], func=AF.Copy),
                              r=[kws], w=[kkpe])
                        yield
                        dma("sp", sq["ckvo"][t0:t0 + tn, :], ckvf[0:tn, :], "o_ckv%d" % kckv[1], r=[kckv])
                        dma("sp", sq["kpeo"][t0:t0 + tn, :], kpef[0:tn, :], "o_kpe%d" % kkpe[1], r=[kkpe])
                        ks8q = (ks8, "q")
                        ks8g = (ks8, "g")

                        def _qgen():
                            P.add("act", lambda e, tn=tn, s8=s8: e.activation(out=junk[0:tn, 0:384], in_=wsb[0:tn, 0:384], func=AF.Square,
                                                                              accum_out=s8[0:tn, 18:19]), r=[kws], w=["junk", ks8q])
                            rstd_chain(s8[:, 18:19], 1, 1.0 / 384, ks8q, tn)
                            P.add("act", lambda e, tn=tn, s8=s8: e.activation(out=qln[0:tn, :], in_=wsb[0:tn, 0:384], func=AF.Identity,
                                                                              scale=s8[0:tn, 18:19]), r=[ks8q, kws], w=["qln"])
                            yield

                            def ev_q(pbv, kb, tn=tn):
                                P.add("dve", lambda e: e.tensor_copy(out=qlT[:, :, 0:tn], in_=pbv[:, 0:3 * tn].rearrange("p (c t) -> p c t", c=3)),
                                      w=[kb, "qlT"])
                            transposes([qln[0:tn, c * 128:(c + 1) * 128] for c in range(3)], tn, ["qln"], ev_q, "qlT")
                            yield
                            qbk = [CFG["B"].get() for _ in range(2)]
                            for g, (c0, n) in enumerate(((0, 480), (480, 288))):
                                def mmq(e, g=g, c0=c0, n=n, tn=tn):
                                    for c in range(3):
                                        ins = e.matmul(qbk[g][0][0:tn, 0:n], lhsT=qlT[:, c, 0:tn], rhs=wuqb[:, c, c0:c0 + n],
                                                       start=(c == 0), stop=(c == 2))
                                    return ins
                                P.add("pe", mmq, r=["qlT", "wuqb"], w=[qbk[g][1]])
                            for g, (c0, n) in enumerate(((0, 480), (480, 288))):
                                P.add("act", lambda e, g=g, c0=c0, n=n, tn=tn: e.activation(out=sqf[0:tn, c0:c0 + n], in_=qbk[g][0][0:tn, 0:n],
                                                                                            func=AF.Square), w=[qbk[g][1], "sqf"])
                            P.add("dve", lambda e, tn=tn, s8=s8: e.tensor_reduce(out=s8[0:tn, 24:32], in_=sqf[0:tn, :].rearrange("p (h d) -> p h d", h=8),
                                                                                 axis=AX.X, op=ALU.add), r=["sqf"], w=[ks8q])
                            yield
                            rstd_chain(s8[:, 24:32], 8, 1.0 / 96, ks8q, tn)
                            for g, (h0, nh) in enumerate(((0, 5), (5, 3))):
                                P.add("dve", lambda e, g=g, h0=h0, nh=nh, tn=tn, s8=s8: e.tensor_tensor(
                                    out=qn[0:tn, h0 * 96:(h0 + nh) * 96].rearrange("p (h d) -> p h d", h=nh),
                                    in0=qbk[g][0][0:tn, 0:nh * 96].rearrange("p (h d) -> p h d", h=nh),
                                    in1=s8[0:tn, 24 + h0:24 + h0 + nh].unsqueeze(2).broadcast_to([tn, nh, 96]), op=ALU.mult),
                                    r=[ks8q], w=[qbk[g][1], "qn"])
                                CFG["B"].rel(qbk[g][2])
                            P.add("dve", lambda e, tn=tn: e.tensor_tensor(out=qn[0:tn, :], in0=qn[0:tn, :], in1=gqnbc[0:tn, :], op=ALU.mult),
                                  r=["gqnbc"], w=["qn"])
                            yield
                            qn3 = qn[0:tn, :].rearrange("p (h d) -> p h d", h=8)
                            qb3 = qb[0:tn, :].rearrange("p (h d) -> p h d", h=8)
                            ta3 = tmpa[0:tn, :].rearrange("p (h d) -> p h d", h=8)
                            tb3 = tmpb[0:tn, :].rearrange("p (h d) -> p h d", h=8)
                            P.add("act", lambda e, qn3=qn3, qb3=qb3: e.activation(out=qb3[:, :, 0:64], in_=qn3[:, :, 0:64], func=AF.Copy),
                                  r=["qn"], w=["qb"])
                            P.add("dve", lambda e, qn3=qn3, ta3=ta3, tab=tab, tn=tn: e.tensor_tensor(
                                out=ta3, in0=qn3[:, :, 64:96], in1=tab[0:tn, 0:32].unsqueeze(1).broadcast_to([tn, 8, 32]), op=ALU.mult),
                                r=["qn", ktab], w=["tmpa"])
                            P.add("dve", lambda e, qn3=qn3, tb3=tb3, tab=tab, tn=tn: e.tensor_tensor(
                                out=tb3[:, :, 0:16], in0=qn3[:, :, 80:96], in1=tab[0:tn, 32:48].unsqueeze(1).broadcast_to([tn, 8, 16]), op=ALU.mult),
                                r=["qn", ktab], w=["tmpb"])
                            P.add("dve", lambda e, qn3=qn3, tb3=tb3, tab=tab, tn=tn: e.tensor_tensor(
                                out=tb3[:, :, 16:32], in0=qn3[:, :, 64:80], in1=tab[0:tn, 48:64].unsqueeze(1).broadcast_to([tn, 8, 16]), op=ALU.mult),
                                r=["qn", ktab], w=["tmpb"])
                            P.add("dve", lambda e, qb3=qb3, ta3=ta3, tb3=tb3: e.tensor_tensor(out=qb3[:, :, 64:96], in0=ta3, in1=tb3, op=ALU.add),
                                  r=["tmpa", "tmpb"], w=["qb"])

                            yield

                            def ev_Q(pbv, kb, tn=tn, QT=QT, col=col):
                                P.add("dve", lambda e: e.tensor_copy(out=QT[:, :, col:col + tn],
                                                                     in_=pbv[0:96, 0:8 * tn].rearrange("p (h t) -> p h t", h=8)),
                                      w=[kb, kQT])
                            transposes([qb[0:tn, h * 96:(h + 1) * 96] for h in range(8)], tn, ["qb"], ev_Q, kQT)

                        def _ggen():
                            P.add("dve", lambda e: e.tensor_copy(out=glrb[0:tn, :], in_=wsb[0:tn, 1696:1712]), r=[kws], w=["glrb"])

                            def ev_g(pbv, kb, tn=tn):
                                P.add("dve", lambda e: e.tensor_copy(out=glrT[:, 0:tn], in_=pbv[0:16, 0:tn]), w=[kb, "glrT"])
                            transposes([glrb[0:tn, :]], tn, ["glrb"], ev_g, "glrT")
                            pz, kz, iz = CFG["B"].get()
                            P.add("pe", lambda e, pz=pz, tn=tn: e.matmul(pz[0:tn, 0:256], lhsT=glrT[:, 0:tn], rhs=wa2b[:, :], start=True, stop=True),
                                  r=["glrT", "wa2b"], w=[kz])
                            P.add("dve", lambda e, pz=pz, tn=tn: e.tensor_tensor(out=zb[0:tn, :], in0=pz[0:tn, 0:256], in1=ba2bc[0:tn, :], op=ALU.add),
                                  r=["ba2bc"], w=[kz, "zb"])
                            CFG["B"].rel(iz)
                            P.add("act", lambda e, tn=tn: e.activation(out=ez[0:tn, :], in_=zb[0:tn, :], func=AF.Exp, scale=-1.0), r=["zb"], w=["ez"])
                            P.add("act", lambda e, tn=tn: e.activation(out=spl[0:tn, :], in_=ez[0:tn, :], func=AF.Ln, bias=1.0), r=["ez"], w=["spl"])
                            yield
                            pc, kc_, ic = CFG["B"].get()
                            P.add("pe", lambda e, pc=pc, tn=tn: e.matmul(pc[0:tn, 0:256], lhsT=trif[0:tn, 0:tn], rhs=spl[0:tn, :], start=True, stop=True),
                                  r=["spl", "trif"], w=[kc_])

                            def mmt(e, pc=pc, tn=tn):
                                for h in range(4):
                                    ins = e.matmul(pc[0:64, 256 + 2 * h:256 + 2 * h + 2], lhsT=spl[0:tn, h * 64:(h + 1) * 64], rhs=onesf[0:tn, 0:2],
                                                   start=True, stop=True)
                                return ins
                            P.add("pe", mmt, r=["spl", "onesf"], w=[kc_])
                            P.add("act", lambda e, pc=pc, tn=tn: e.activation(out=eb[0:tn, :], in_=pc[0:tn, 0:256], func=AF.Exp, scale=-1.0 / 16,
                                                                              bias=math.log(0.125)), w=[kc_, "eb"])
                            P.add("act", lambda e, pc=pc, tn=tn: e.activation(out=enb[0:tn, :], in_=pc[0:tn, 0:256], func=AF.Exp, scale=1.0 / 16),
                                  w=[kc_, "enb"])
                            P.add("act", lambda e, pc=pc: e.activation(out=gfm[:, :], in_=pc[0:64, 256:264].rearrange("p (a b) -> p a b", b=2)[:, :, 0],
                                                                       func=AF.Exp, scale=-1.0 / 16), w=[kc_, "gfm"])
                            CFG["B"].rel(ic)
                            P.add("dve", lambda e, tn=tn: e.tensor_tensor(out=qt[0:tn, :], in0=wsb[0:tn, 672:928], in1=eb[0:tn, :], op=ALU.mult),
                                  r=["eb", kws], w=["qt"])
                            P.add("dve", lambda e, tn=tn: e.tensor_tensor(out=kt[0:tn, :], in0=wsb[0:tn, 928:1184], in1=enb[0:tn, :], op=ALU.mult),
                                  r=["enb", kws], w=["kt"])
                            P.add("act", lambda e: e.activation(out=vb[0:tn, :], in_=wsb[0:tn, 1184:1696], func=AF.Copy), r=[kws], w=["vb"])
                            yield

                            yield

                            def ev_qk(pbv, kb, tn=tn):
                                P.add("dve", lambda e: e.tensor_copy(out=qkT[:, :, 0:tn], in_=pbv[0:64, 0:8 * tn].rearrange("p (c t) -> p c t", c=8)),
                                      w=[kb, "qkT"])
                            transposes([qt[0:tn, h * 64:(h + 1) * 64] for h in range(4)] + [kt[0:tn, h * 64:(h + 1) * 64] for h in range(4)],
                                       tn, ["qt", "kt"], ev_qk, "qkT")
                            yield
                            pa, ka, ia = CFG["B"].get()

                            def mma(e, pa=pa, tn=tn):
                                for h in range(4):
                                    ins = e.matmul(pa[0:tn, h * tn:(h + 1) * tn], lhsT=qkT[:, 4 + h, 0:tn], rhs=qkT[:, h, 0:tn],
                                                   start=True, stop=True)
                                return ins
                            P.add("pe", mma, r=["qkT"], w=[ka])
                            P.add("dve", lambda e, pa=pa, tn=tn: e.tensor_tensor(
                                out=ATb[0:tn, :, 0:tn], in0=pa[0:tn, 0:4 * tn].rearrange("p (h t) -> p h t", h=4),
                                in1=trif[0:tn, 0:tn].unsqueeze(1).broadcast_to([tn, 4, tn]), op=ALU.mult), r=["trif"], w=[ka, "ATb"])
                            CFG["B"].rel(ia)
                            yield
                            po, ko, io = CFG["B"].get()

                            def mmo(e, po=po, tn=tn):
                                for h in range(4):
                                    e.matmul(po[0:tn, h * 128:(h + 1) * 128], lhsT=ATb[0:tn, h, 0:tn], rhs=vb[0:tn, h * 128:(h + 1) * 128],
                                             start=True, stop=False)
                                    ins = e.matmul(po[0:tn, h * 128:(h + 1) * 128], lhsT=qkT[:, h, 0:tn], rhs=Sb[:, h, :],
                                                   start=False, stop=True)
                                return ins
                            P.add("pe", mmo, r=["ATb", "vb", "qkT", "Sb"], w=[ko])
                            pu, ku, iu = CFG["B"].get()

                            def mmu(e, pu=pu, tn=tn):
                                for h in range(4):
                                    ins = e.matmul(pu[0:64, h * 128:(h + 1) * 128], lhsT=kt[0:tn, h * 64:(h + 1) * 64],
                                                   rhs=vb[0:tn, h * 128:(h + 1) * 128], start=True, stop=True)
                                return ins
                            P.add("pe", mmu, r=["kt", "vb"], w=[ku])

                            yield

                            def upd(e, pu=pu):
                                for pp in range(4):
                                    ins = e.tensor_scalar(out=Sg_[:, pp, :], in0=Sf[:, pp, :], scalar1=gfm[:, pp:pp + 1], scalar2=None, op0=ALU.mult)
                                return ins
                            P.add("dve", upd, r=["Sf", "gfm"], w=["Sg_"])

                            def upd2(e, pu=pu):
                                for pp in range(4):
                                    ins = e.scalar_tensor_tensor(out=Sf[:, pp, :], in0=pu[0:64, pp * 128:(pp + 1) * 128], scalar=gfm[:, pp:pp + 1],
                                                                 in1=Sg_[:, pp, :], op0=ALU.mult, op1=ALU.add)
                                return ins
                            P.add("dve", upd2, r=["Sg_", "gfm"], w=[ku, "Sf"])
                            CFG["B"].rel(iu)
                            P.add("act", lambda e: e.activation(out=Sb[:], in_=Sf[:], func=AF.Copy), r=["Sf"], w=["Sb"])

                            yield

                            def sqo(e, po=po, tn=tn, s8=s8):
                                for h in range(4):
                                    ins = e.activation(out=junk[0:tn, h * 128:(h + 1) * 128], in_=po[0:tn, h * 128:(h + 1) * 128], func=AF.Square,
                                                       accum_out=s8[0:tn, 32 + h:33 + h])
                                return ins
                            P.add("act", sqo, w=[ko, "junk", ks8g])
                            rstd_chain(s8[:, 32:36], 4, 1.0 / 128, ks8g, tn)
                            yield
                            P.add("dve", lambda e, po=po, tn=tn, s8=s8: e.tensor_tensor(
                                out=tmpo[0:tn, :].rearrange("p (h d) -> p h d", h=4), in0=po[0:tn, :].rearrange("p (h d) -> p h d", h=4),
                                in1=s8[0:tn, 32:36].unsqueeze(2).broadcast_to([tn, 4, 128]), op=ALU.mult), r=[ks8g], w=[ko, "tmpo"])
                            CFG["B"].rel(io)
                            P.add("act", lambda e: e.activation(out=sg[0:tn, :], in_=wsb[0:tn, 1712:2224], func=AF.Silu), r=[kws], w=["sg"])
                            P.add("dve", lambda e, tn=tn: e.tensor_tensor(out=gated[0:tn, :], in0=tmpo[0:tn, :], in1=sg[0:tn, :], op=ALU.mult),
                                  r=["tmpo", "sg"], w=["gated"])

                            def ev_m(pbv, kb, tn=tn, col=col):
                                P.add("dve", lambda e: e.tensor_copy(out=mxb[:, 4:8, col:col + tn],
                                                                     in_=pbv[:, 0:4 * tn].rearrange("p (c t) -> p c t", c=4)), w=[kb, kmx])
                            transposes([gated[0:tn, h * 128:(h + 1) * 128] for h in range(4)], tn, ["gated"], ev_m, kmx)
                            yield

                        def _rr(gens):
                            gens = list(gens)
                            while gens:
                                for g_ in list(gens):
                                    try:
                                        next(g_)
                                    except StopIteration:
                                        gens.remove(g_)
                                yield
                        if SUBRR:
                            yield from _rr([_qgen(), keyside(sq, ti, tn, ckvf, kckv, kpef, kkpe, tab, ktab, col), _ggen()])
                        else:
                            yield from _qgen()
                            yield from keyside(sq, ti, tn, ckvf, kckv, kpef, kkpe, tab, ktab, col)
                            yield from _ggen()
                    def spill():
                        tiA = (past + q0) // 128
                        dma("sp", sq["kTs"][:, :, tiA * 128:tiA * 128 + NQB].rearrange("h d t -> d h t"), KTb[:, :, 0:NQB], "kTw",
                            r=["KTb"], w=[("kT", sq["name"], t) for t in range(tiA, tiA + ntile)])

                    def lockstep(a, b, lag):
                        done_a = done_b = False
                        step = 0
                        while not (done_a and done_b):
                            if not done_a:
                                try:
                                    next(a)
                                except StopIteration:
                                    done_a = True
                            if (step >= lag or done_a) and not done_b:
                                try:
                                    next(b)
                                except StopIteration:
                                    done_b = True
                            step += 1

                    def _run_gen():
                        for it in range(ntile):
                            yield from _p1tile(it)
                        spill()

                    if gen:
                        return _run_gen()
                    pending = list(range(ntile))
                    running = []
                    since = 10 ** 9
                    while pending or running:
                        if pending and len(running) < 2 and since >= LAG:
                            running.append(_p1tile(pending.pop(0)))
                            since = 0
                        for g_ in list(running):
                            try:
                                next(g_)
                            except StopIteration:
                                running.remove(g_)
                        since += 1
                    spill()
                    return None

                blocks = list(range(0, T, NQB))
                infos = [dict() for _ in blocks]
                doneA = set()
                for bi, q0 in enumerate(blocks):
                    inf = infos[bi]
                    if bi not in doneA:
                        set_mode(False)
                        _p1blk(q0, inf)
                    inter = (bi >= BTH and bi + 1 < len(blocks))
                    if inter:
                        set_mode(True)
                        bg = _p1blk(blocks[bi + 1], infos[bi + 1], True)
                        attention(sq, q0, NQB, inf["QT"], inf["kQT"], inf["mxb"], inf["kmx"], bg, 108)
                        for _ in bg:
                            pass
                        doneA.add(bi + 1)
                    else:
                        set_mode(False)
                        attention(sq, q0, NQB, inf["QT"], inf["kQT"], inf["mxb"], inf["kmx"], None)
                    dma("sp", sq["mxs"].rearrange("(c p) t -> p c t", p=128)[:, :, q0:q0 + NQB], inf["mxb"][:, :, 0:NQB], "mxw%d" % inf["kmx"][1],
                        r=[inf["kmx"]], w=[("mxs", sq["name"], q0)])
                dma("sp", sq["glao"].rearrange("h k v -> k h v"), Sf[:, :, :], "o_S", r=["Sf"])
            for sq in seqs[:max(0, STAGE - 1)]:
                _p1seq(sq)
            P.emit(nc, semget)

        with contextlib.ExitStack() as st:
            woutf = sb(st, "woutf", [128, 2, D])
            woutb = sb(st, "woutb", [128, 8, D], BF16)
            gglafm = sb(st, "gglafm", [128, 1])
            wcv = sb(st, "wcv", [128, NJ, 3])
            bcv = sb(st, "bcv", [128, NJ])
            gbc = sb(st, "gbc", [128, 2048])
            hist = sb(st, "hist", [128, NJ, 2])
            mxr = ring(st, "mx", [128, 8, 512], BF16, 2)
            xr = ring(st, "x2t", [128, D], F32, 2)
            x2r = ring(st, "x2b", [128, 4, D], F32, 2)
            tmpr = ring(st, "tm", [128, 512], F32, 2)
            junk = sb(st, "junk2", [128, D], BF16)
            xsr = ring(st, "xs2", [128, D], BF16, 2)
            st8 = ring(st, "s2", [128, 8], F32, 2)
            h2r = ring(st, "h2T", [128, 8, 512], BF16, 2)
            war = ring(st, "wug", [128, 2, 8, 128], BF16, 3)
            abr = ring(st, "ab", [128, 514], F32, 2)
            cbr = ring(st, "cb", [128, 512], F32, 2)
            glr = ring(st, "gl", [128, 512], F32, 2)
            uT = sb(st, "uT", [128, NJ, 512], BF16)
            wdr = ring(st, "wd", [128, 512], BF16, 4)
            ybr = ring(st, "yb", [128, D], F32, 2)
            cvo = sb(st, "cvo", [128, 2 * NJ])
            cvt = sb(st, "cvt", [2 * NJ, 128])
            nhalf = sb(st, "nhalf", [128, 1])
            P.add("dve", lambda e: e.memset(nhalf[:], -0.5), w=["nhalf"])
            dma("sp", gglafm[:], I["ggla_fm"], "l_ggla", w=["gglafm"])
            dma("sp", wcv[:], I["wconv_fm"], "l_wcv", w=["wcv"])
            dma("sp", bcv[:], I["bconv_fm"], "l_bcv", w=["bcv"])
            for c in range(0, 8, 2):
                dma("sp", woutf[:], I["wout"][c * 128:(c + 2) * 128, :].rearrange("(c p) n -> p c n", p=128), "l_wout", r=[], w=["woutf"])
                for cc in range(2):
                    if c + cc < 4:
                        P.add("dve", lambda e, c=c, cc=cc: e.tensor_copy(out=woutb[:, c + cc, :], in_=woutf[:, cc, :]), r=["woutf"], w=["woutb"])
                    else:
                        P.add("dve", lambda e, c=c, cc=cc: e.tensor_scalar(out=woutb[:, c + cc, :], in0=woutf[:, cc, :], scalar1=gglafm[:, 0:1],
                                                                           scalar2=None, op0=ALU.mult), r=["woutf", "gglafm"], w=["woutb"])
            def _p2seq(sq):
                s = sq["s"]
                T, past = sq["T"], sq["past"]
                dma("sp", gbc[:], gbcs[s], "l_gbc", r=[("gbcs", s)], w=["gbc"])
                if past:
                    dma("sp", hist[:], I["sconv"], "l_hist", w=["hist"])
                else:
                    P.add("dve", lambda e: e.memset(hist[:], 0.0), w=["hist"])
                NQB = min(512, T)
                ntile = (NQB + 127) // 128

                def _front(q0, info):
                    mx, kmx = mxr.next()
                    x2b, kx2 = x2r.next()
                    h2T, kh2 = h2r.next()
                    info.update(x2b=x2b, kx2=kx2, h2T=h2T, kh2=kh2)
                    dma("sp", mx[:, :, 0:NQB], sq["mxs"].rearrange("(c p) t -> p c t", p=128)[:, :, q0:q0 + NQB], "l_mx%d" % kmx[1],
                        r=[("mxs", sq["name"], q0)], w=[kmx])
                    yield

                    def _p2tile(it):
                        t0 = q0 + it * 128
                        tn = min(128, T - t0)
                        col = it * 128
                        xt, kx = xr.next()
                        xs_, kxs = xsr.next()
                        s8, ks8 = st8.next()
                        dma("sp", xt[0:tn, :], sq["x"][t0:t0 + tn, :], "l_xx%d" % kx[1], w=[kx])
                        for half in range(2):
                            pm, km, im = PS.get()
                            tm, ktm = tmpr.next()

                            def mmw(e, pm=pm, half=half, mx=mx, col=col, tn=tn):
                                for c in range(8):
                                    ins = e.matmul(pm[0:tn, :], lhsT=mx[:, c, col:col + tn], rhs=woutb[:, c, half * 512:(half + 1) * 512],
                                                   start=(c == 0), stop=(c == 7))
                                return ins
                            P.add("pe", mmw, r=[kmx, "woutb"], w=[km])
                            P.add("dve", lambda e, pm=pm, tm=tm, half=half, tn=tn: e.tensor_tensor(
                                out=tm[0:tn, :], in0=pm[0:tn, :], in1=gbc[0:tn, half * 512:(half + 1) * 512], op=ALU.mult),
                                r=["gbc"], w=[km, ktm])
                            PS.rel(im)
                            P.add("dve", lambda e, tm=tm, half=half, tn=tn, it=it, xt=xt: e.tensor_tensor(
                                out=x2b[0:tn, it, half * 512:(half + 1) * 512], in0=tm[0:tn, :], in1=xt[0:tn, half * 512:(half + 1) * 512],
                                op=ALU.add), r=[ktm, kx], w=[(kx2, it)])
                            yield
                        P.add("act", lambda e, it=it, tn=tn, s8=s8: e.activation(out=junk[0:tn, :], in_=x2b[0:tn, it, :], func=AF.Square,
                                                                                 accum_out=s8[0:tn, 0:1]), r=[(kx2, it)], w=["junk2", ks8])
                        P.add("dve", lambda e, tn=tn, s8=s8: e.tensor_scalar(out=s8[0:tn, 0:1], in0=s8[0:tn, 0:1], scalar1=1.0 / D, scalar2=EPS,
                                                                             op0=ALU.mult, op1=ALU.add), w=[ks8])
                        P.add("pool", lambda e, tn=tn, s8=s8: e.tensor_tensor(out=s8[0:tn, 0:1], in0=s8[0:tn, 0:1], in1=nhalf[0:tn, 0:1], op=ALU.pow),
                              r=["nhalf"], w=[ks8])
                        yield
                        P.add("act", lambda e, it=it, tn=tn, s8=s8, xs_=xs_: e.activation(out=xs_[0:tn, :], in_=x2b[0:tn, it, :], func=AF.Identity,
                                                                                          scale=s8[0:tn, 0:1]), r=[(kx2, it), ks8], w=[kxs])
                        yield
                        yield
                        yield
                        pb, kb, ib = PS.get()
                        pbv = bf(pb[:, :])

                        def tr(e, pbv=pbv, xs_=xs_, tn=tn):
                            for c in range(8):
                                ins = e.transpose(out=pbv[:, c * tn:(c + 1) * tn], in_=xs_[0:tn, c * 128:(c + 1) * 128], identity=identb[0:tn, 0:tn])
                            return ins
                        P.add("pe", tr, r=[kxs, "identb"], w=[kb])
                        yield
                        yield

                        def evh(e, pbv=pbv, tn=tn, col=col, s=s):
                            for c in range(8):
                                ins = e.tensor_scalar(out=h2T[:, c, col:col + tn], in0=pbv[:, c * tn:(c + 1) * tn], scalar1=mul2[:, s, c:c + 1],
                                                      scalar2=mods[:, 24 + c, s:s + 1], op0=ALU.mult, op1=ALU.add)
                            return ins
                        P.add("dve", evh, r=["mods", ("mul", id(mul2), s)], w=[kb, kh2])
                        PS.rel(ib)
                        yield
                    for it in range(ntile):
                        yield from _p2tile(it)

                def _p2blk(q0, info, bg):
                    x2b, kx2, h2T, kh2 = info["x2b"], info["kx2"], info["h2T"], info["kh2"]
                    nopp = 3 * NJ
                    tick = [0, 0]

                    def pump():
                        if bg is None:
                            return
                        tick[0] += 1
                        want = (tick[0] * 44) // nopp
                        while tick[1] < want:
                            tick[1] += 1
                            try:
                                next(bg)
                            except StopIteration:
                                return

                    def _c1(j):
                        wug, kwug = war.next()
                        dma("sp", wug[:, 0, :, :], wups[j], "l_wu%d" % kwug[1], r=[("wups", c) for c in range(8)], w=[kwug])
                        dma("sp", wug[:, 1, :, :], wups[NJ + j], "l_wg%d" % kwug[1], r=[("wups", c) for c in range(8)], w=[kwug])
                        pa, ka, ia = PS.get()
                        pg, kg_, ig = PS.get()
                        for (pp_, kk_, wi) in ((pa, ka, 0), (pg, kg_, 1)):
                            def mmf(e, pp_=pp_, wi=wi, wug=wug):
                                for c in range(8):
                                    ins = e.matmul(pp_[:, 0:NQB], lhsT=wug[:, wi, c, :], rhs=h2T[:, c, 0:NQB], start=(c == 0), stop=(c == 7))
                                return ins
                            P.add("pe", mmf, r=[kwug, kh2], w=[kk_])
                        ab, kab = abr.next()
                        cb, kcb = cbr.next()
                        gl, kgl = glr.next()
                        P.add("dve", lambda e, ab=ab, j=j: e.tensor_copy(out=ab[:, 0:2], in_=hist[:, j, :]), r=[("hist", j), "hist"], w=[kab])
                        P.add("act", lambda e, ab=ab, pa=pa: e.activation(out=ab[:, 2:2 + NQB], in_=pa[:, 0:NQB], func=AF.Copy), w=[ka, kab])
                        P.add("dve", lambda e, ab=ab, j=j: e.tensor_copy(out=hist[:, j, :], in_=ab[:, NQB:NQB + 2]), r=[kab], w=[("hist", j)])
                        P.add("act", lambda e, pa=pa, cb=cb, j=j: e.activation(out=cb[:, 0:NQB], in_=pa[:, 0:NQB], func=AF.Identity,
                                                                               scale=wcv[:, j, 2:3], bias=bcv[:, j:j + 1]),
                              r=["wcv", "bcv"], w=[ka, kcb])
                        PS.rel(ia)
                        P.add("dve", lambda e, ab=ab, cb=cb, j=j: e.scalar_tensor_tensor(out=cb[:, 0:NQB], in0=ab[:, 1:1 + NQB], scalar=wcv[:, j, 1:2],
                                                                                         in1=cb[:, 0:NQB], op0=ALU.mult, op1=ALU.add),
                              r=[kab, "wcv"], w=[kcb])
                        P.add("dve", lambda e, ab=ab, cb=cb, j=j: e.scalar_tensor_tensor(out=cb[:, 0:NQB], in0=ab[:, 0:NQB], scalar=wcv[:, j, 0:1],
                                                                                         in1=cb[:, 0:NQB], op0=ALU.mult, op1=ALU.add),
                              r=[kab, "wcv"], w=[kcb])
                        P.add("act", lambda e, cb=cb, gl=gl: e.activation(out=gl[:, 0:NQB], in_=cb[:, 0:NQB], func=AF.Gelu_apprx_tanh), r=[kcb], w=[kgl])
                        P.add("dve", lambda e, gl=gl, pg=pg, j=j: e.tensor_tensor(out=uT[:, j, 0:NQB], in0=pg[:, 0:NQB], in1=gl[:, 0:NQB], op=ALU.mult),
                              r=[kgl], w=[kg_, ("uT", j)])
                        PS.rel(ig)
                    for j in range(NJ):
                        _c1(j)
                        pump()
                    def _c2(half):
                        yb4 = [PS.get() for _ in range(ntile)]
                        for j in range(NJ):
                            wd, kwd = wdr.next()
                            dma("sp", wd[:], wdowns[j * 128:(j + 1) * 128, half * 512:(half + 1) * 512], "l_wd%d" % kwd[1], r=["wdowns"], w=[kwd])
                            for it in range(ntile):
                                tn = min(128, T - (q0 + it * 128))
                                P.add("pe", lambda e, it=it, tn=tn, j=j, wd=wd: e.matmul(
                                    yb4[it][0][0:tn, :], lhsT=uT[:, j, it * 128:it * 128 + tn], rhs=wd[:, :], start=(j == 0), stop=(j == NJ - 1)),
                                    r=[("uT", j), kwd], w=[yb4[it][1]])
                            pump()
                        for it in range(ntile):
                            tn = min(128, T - (q0 + it * 128))
                            tm, ktm = tmpr.next()
                            P.add("dve", lambda e, it=it, tn=tn, tm=tm, half=half: e.tensor_tensor(
                                out=tm[0:tn, :], in0=yb4[it][0][0:tn, :], in1=gbc[0:tn, 1024 + half * 512:1024 + (half + 1) * 512], op=ALU.mult),
                                r=["gbc"], w=[yb4[it][1], ktm])
                            PS.rel(yb4[it][2])
                            P.add("dve", lambda e, it=it, tn=tn, tm=tm, half=half: e.tensor_tensor(
                                out=x2b[0:tn, it, half * 512:(half + 1) * 512], in0=tm[0:tn, :], in1=x2b[0:tn, it, half * 512:(half + 1) * 512],
                                op=ALU.add), r=[ktm], w=[(kx2, it)])
                    for half in range(2):
                        _c2(half)
                    for it in range(ntile):
                        t0 = q0 + it * 128
                        tn = min(128, T - t0)
                        dma(XQ, sq["y"][t0:t0 + tn, :], x2b[0:tn, it, :], "o_y%d_%d" % (kx2[1], it), r=[(kx2, it)])
                    if bg is not None:
                        for _ in bg:
                            pass
                blocks = list(range(0, T, NQB))
                infos = [dict() for _ in blocks]
                for _ in _front(blocks[0], infos[0]):
                    pass
                for bi, q0 in enumerate(blocks):
                    bg = _front(blocks[bi + 1], infos[bi + 1]) if bi + 1 < len(blocks) else None
                    _p2blk(q0, infos[bi], bg)
                P.add("dve", lambda e: e.tensor_copy(out=cvo[:].rearrange("p (r j) -> p r j", r=2), in_=hist[:].rearrange("p j r -> p r j")),
                      r=["hist"] + [("hist", j) for j in range(NJ)], w=["cvo"])
                pb, kb, ib = PS.get()
                P.add("pe", lambda e, pb=pb: e.transpose(out=pb[0:2 * NJ, 0:128], in_=cvo[:, :], identity=identf[:, :]), r=["cvo", "identf"], w=[kb])
                P.add("act", lambda e, pb=pb: e.activation(out=cvt[:, :], in_=pb[0:2 * NJ, 0:128], func=AF.Copy), w=[kb, "cvt"])
                PS.rel(ib)
                for r_ in range(2):
                    dma("sp", sq["convo"][r_].rearrange("(j p) -> j p", p=128), cvt[r_ * NJ:(r_ + 1) * NJ, :], "o_conv%d" % r_, r=["cvt"])
            for sq in seqs[:max(0, STAGE - 3)]:
                _p2seq(sq)
            P.emit(nc, semget)
        P.emit(nc, semget, final=True)
    return nc


_CACHE = {}


def _consts(TP):
    npos = max(TP, PAST + TS)
    half = 16
    inv = (1.0 / (np.float32(10000.0) ** (np.arange(half, dtype=np.float32) / np.float32(half)))).astype(np.float32)
    ang = np.arange(npos, dtype=np.float32)[:, None] * inv[None, :]
    cos, sin = np.cos(ang).astype(np.float32), np.sin(ang).astype(np.float32)
    tab = np.concatenate([cos, cos, -sin, sin], axis=1).astype(np.float32)
    ident = np.eye(128, dtype=np.float32)
    tri = np.triu(np.ones((128, 128), np.float32))
    ones = np.ones((128, 128), np.float32)
    sel2 = np.zeros((2, 2, 128), np.float32)
    sel2[0, 0, :] = 1.0
    sel2[1, 1, :] = 1.0
    return dict(tab=tab, ident=ident, tri=tri, ones=ones, sel2=sel2)


def kernel(x_prompt, x_sample, c_prompt, c_sample, cache_ckv, cache_kpe, state_gla, state_ffn_conv,
           w_ada, b_ada, g_norm1, w_in, g_qa, w_uq, g_qn, g_kva, w_ukv, g_kn, w_a2, b_a2, g_gla,
           w_out, g_norm2, w_up, w_conv, b_conv, w_down):
    f = lambda a: np.ascontiguousarray(np.asarray(a, dtype=np.float32))
    x_prompt = f(x_prompt)
    B, TP, _ = x_prompt.shape
    import os
    if TP not in _CACHE:
        _CACHE[TP] = build(TP, int(os.environ.get('KSTAGE', '9')))
    nc = _CACHE[TP]
    cs = _consts(TP)
    fm = lambda v, n: f(np.asarray(v, np.float32).reshape(n, 128).T)
    b_ada0 = np.asarray(b_ada, np.float32)[0]
    shared = dict(
        wada=f(w_ada[0]), bada_fm=fm(b_ada0, 48),
        bada2=f(np.tile(np.concatenate([b_ada0[2048:3072], b_ada0[5120:6144]])[None, :], (2, 1))),
        g1_fm=fm(g_norm1[0], 8), g2_fm=fm(g_norm2[0], 8), win=f(w_in[0]), gqa_fm=fm(g_qa[0], 3), wuq=f(w_uq[0]),
        gqn8=f(np.tile(np.asarray(g_qn[0], np.float32), 8)[None, :]), gkva=f(np.asarray(g_kva[0])[None, :]), wukv=f(w_ukv[0]),
        gkn=f(np.asarray(g_kn[0])[None, :]), wa2=f(w_a2[0]), ba2=f(np.asarray(b_a2[0])[None, :]), ggla_fm=fm(g_gla[0], 1),
        wout=f(w_out[0]), wup=f(w_up[0]),
        wconv_fm=f(np.asarray(w_conv[0], np.float32).reshape(3, NJ, 128).transpose(2, 1, 0)),
        bconv_fm=fm(b_conv[0], NJ), wdown=f(w_down[0]),
        ident=cs["ident"], tri=cs["tri"], ones=cs["ones"], tab=cs["tab"], sel2=cs["sel2"])
    in_maps = []
    for b in range(B):
        c2 = np.stack([np.asarray(c_prompt[b], np.float32), np.asarray(c_sample[b], np.float32)], axis=-1)
        m = dict(shared)
        m.update(
            xp=x_prompt[b], xs=f(x_sample[b]), cT=f(c2.reshape(8, 128, 2).transpose(1, 0, 2)),
            cckv=f(cache_ckv[0, b]), ckpe=f(cache_kpe[0, b]), sgla=f(state_gla[0, b]),
            sconv=f(np.asarray(state_ffn_conv[0, b], np.float32).reshape(2, NJ, 128).transpose(2, 1, 0)))
        in_maps.append(m)
    res = run_bass_kernel_spmd(nc, in_maps, core_ids=list(range(B)))
    R = res.results
    g = lambda k: np.stack([np.asarray(R[b][k], np.float32) for b in range(B)], axis=0)
    return (g("yp"), g("ys"), g("ckvp")[None], g("kpep")[None], g("glap")[None], g("convp")[None],
            g("ckvs")[None], g("kpes")[None], g("glas")[None], g("convs")[None])
```

```python
import contextlib
import math
import numpy as np
import concourse.bass as bass
import concourse.mybir as mybir
from concourse.bass_utils import run_bass_kernel_spmd

F32, BF16 = mybir.dt.float32, mybir.dt.bfloat16
ALU = mybir.AluOpType
AF = mybir.ActivationFunctionType
AX = mybir.AxisListType

D = 1024
FF = 2816
NJ = FF // 128
EPS = 1e-6
PAST = 1024
TS = 64


class Op:
    __slots__ = ("eng", "fn", "deps", "idx", "sig", "sem", "val", "dma", "ndma", "emitted")


class Prog:
    ENGS = ("pe", "act", "dve", "pool", "sp")

    def __init__(self):
        self.ops = []
        self.lw = {}
        self.rd = {}
        self.cnt = {e: 0 for e in self.ENGS}
        self.dcnt = {}
        self.waited = {e: {} for e in self.ENGS}
        self.sems = {}
        self.start = 0
        self.prev_final = {}

    def add(self, eng, fn, r=(), w=(), dma=None, ndma=1):
        import os
        if len(self.ops) >= int(os.environ.get('KMAXOPS', '100000000')):
            return None
        o = Op()
        o.eng, o.fn, o.dma, o.ndma = eng, fn, dma, ndma
        o.sig = False
        o.sem = None
        o.val = 0
        o.idx = len(self.ops)
        o.emitted = False
        deps = set()
        for k in r:
            p = self.lw.get(k)
            if p is not None:
                deps.add(p)
        for k in w:
            p = self.lw.get(k)
            if p is not None:
                deps.add(p)
            for q in self.rd.get(k, ()):
                deps.add(q)
        for k in r:
            self.rd.setdefault(k, []).append(o)
        for k in w:
            self.lw[k] = o
            self.rd[k] = []
        deps.discard(o)
        o.deps = deps
        self.ops.append(o)
        return o

    def emit(self, nc, semget, final=False):
        ops = self.ops[self.start:]
        for o in ops:
            for d in o.deps:
                if d.dma is None and not (d.eng == "pe" and o.eng == "pe"):
                    if d.emitted and not d.sig:
                        raise RuntimeError("dependency on already-emitted unsignalled op")
                    d.sig = True
        live = set(self.lw.values())
        for lst in self.rd.values():
            live.update(lst)
        for o in ops:
            if o.dma is None and o in live:
                o.sig = True
        for o in ops:
            if o.dma is not None:
                self.dcnt[o.dma] = self.dcnt.get(o.dma, 0) + 16 * o.ndma
                o.sem = ("d", o.dma)
                o.val = self.dcnt[o.dma]
            elif o.sig:
                self.cnt[o.eng] += 1
                o.sem = ("e", o.eng)
                o.val = self.cnt[o.eng]
        for sk in set(o.sem for o in ops if o.sem is not None):
            if sk not in self.sems:
                self.sems[sk] = semget(sk)

        prev_final = dict(self.prev_final)

        def body(ename):
            def run(e):
                waited = self.waited[ename]
                for sk, v in prev_final.items():
                    if waited.get(sk, 0) < v:
                        e.wait_ge(self.sems[sk], v)
                        waited[sk] = v
                for o in ops:
                    if o.eng != ename:
                        continue
                    need = {}
                    for d in o.deps:
                        if d.dma is None and d.eng == "pe" and ename == "pe":
                            continue
                        if d.sem is None:
                            raise RuntimeError("dep without sem")
                        if need.get(d.sem, 0) < d.val:
                            need[d.sem] = d.val
                    for sk, v in need.items():
                        if waited.get(sk, 0) < v:
                            e.wait_ge(self.sems[sk], v)
                            waited[sk] = v
                    if o.dma is not None:
                        o.fn(e, self.sems[o.sem])
                    else:
                        ins = o.fn(e)
                        if o.sig:
                            ins.then_inc(self.sems[o.sem], 1)
                    o.emitted = True
                if final and ename == "sp":
                    for sk, v in self.dcnt.items():
                        e.wait_ge(self.sems[("d", sk)], v)
                    for en in self.ENGS:
                        if self.cnt[en] > 0 and ("e", en) in self.sems:
                            e.wait_ge(self.sems[("e", en)], self.cnt[en])
            return run

        with nc.Block() as block:
            block.sync(body("sp"))
            block.tensor(body("pe"))
            block.scalar(body("act"))
            block.vector(body("dve"))
            block.gpsimd(body("pool"))
        self.start = len(self.ops)
        self.prev_final = {}
        for sk, v in self.dcnt.items():
            self.prev_final[("d", sk)] = v
        for en in self.ENGS:
            if self.cnt[en] > 0 and ("e", en) in self.sems:
                self.prev_final[("e", en)] = self.cnt[en]


class Ring:
    def __init__(self, tensors, name):
        self.t = tensors
        self.name = name
        self.i = 0

    def next(self):
        k = self.i % len(self.t)
        self.i += 1
        return self.t[k], (self.name, k)


class PsumAlloc:
    CLOCK = [lambda: 0]

    def __init__(self, banks, base=0):
        self.banks = banks
        self.base = base
        self.held = set()
        self.last = {k: -1 - (len(banks) - k) for k in range(len(banks))}

    def get(self):
        free = [k for k in range(len(self.banks)) if k not in self.held]
        if not free:
            raise RuntimeError("out of PSUM banks")
        k = min(free, key=lambda j: self.last[j])
        self.held.add(k)
        return self.banks[k], ("ps", self.base + k), k

    def rel(self, k):
        self.held.discard(k)
        self.last[k] = PsumAlloc.CLOCK[0]()


def bf(ap):
    return ap.bitcast(BF16)


import os as _os
LAG = int(_os.environ.get('KLAG', '5'))
SUBRR = int(_os.environ.get('KSUBRR', '1'))
XQ = _os.environ.get('KXQ', 'pool')
BTH = int(_os.environ.get('KBTH', '99'))


def build(TP, STAGE=9):
    nc = bass.Bass("TRN2", target_bir_lowering=False)
    P = Prog()
    PsumAlloc.CLOCK[0] = lambda: len(P.ops)
    stk = contextlib.ExitStack()

    def din(name, shape, dt=F32):
        return nc.dram_tensor(name, list(shape), dt, kind="ExternalInput").ap()

    def dout(name, shape, dt=F32):
        return nc.dram_tensor(name, list(shape), dt, kind="ExternalOutput").ap()

    def dscr(name, shape, dt=BF16):
        return nc.dram_tensor(name, list(shape), dt, kind="Internal").ap()

    I = {}
    I["xp"] = din("xp", [TP, D])
    I["xs"] = din("xs", [TS, D])
    I["cT"] = din("cT", [128, 8, 2])
    I["cckv"] = din("cckv", [PAST, 256])
    I["ckpe"] = din("ckpe", [PAST, 32])
    I["sgla"] = din("sgla", [4, 64, 128])
    I["sconv"] = din("sconv", [128, NJ, 2])
    I["wada"] = din("wada", [D, 6 * D])
    I["bada_fm"] = din("bada_fm", [128, 48])
    I["bada2"] = din("bada2", [2, 2048])
    I["g1_fm"] = din("g1_fm", [128, 8])
    I["g2_fm"] = din("g2_fm", [128, 8])
    I["win"] = din("win", [D, 2224])
    I["gqa_fm"] = din("gqa_fm", [128, 3])
    I["wuq"] = din("wuq", [384, 768])
    I["gqn8"] = din("gqn8", [1, 768])
    I["gkva"] = din("gkva", [1, 256])
    I["wukv"] = din("wukv", [256, 1024])
    I["gkn"] = din("gkn", [1, 96])
    I["wa2"] = din("wa2", [16, 256])
    I["ba2"] = din("ba2", [1, 256])
    I["ggla_fm"] = din("ggla_fm", [128, 1])
    I["wout"] = din("wout", [D, D])
    I["wup"] = din("wup", [D, 2 * FF])
    I["wconv_fm"] = din("wconv_fm", [128, NJ, 3])
    I["bconv_fm"] = din("bconv_fm", [128, NJ])
    I["wdown"] = din("wdown", [FF, D])
    I["ident"] = din("ident", [128, 128])
    I["tri"] = din("tri", [128, 128])
    I["ones"] = din("ones", [128, 128])
    I["tab"] = din("tab", [max(TP, PAST + TS), 64])
    I["sel2"] = din("sel2", [2, 2, 128])
    O = {}
    O["yp"] = dout("yp", [TP, D])
    O["ys"] = dout("ys", [TS, D])
    O["ckvp"] = dout("ckvp", [TP, 256])
    O["kpep"] = dout("kpep", [TP, 32])
    O["glap"] = dout("glap", [4, 64, 128])
    O["convp"] = dout("convp", [2, FF])
    O["ckvs"] = dout("ckvs", [TS, 256])
    O["kpes"] = dout("kpes", [TS, 32])
    O["glas"] = dout("glas", [4, 64, 128])
    O["convs"] = dout("convs", [2, FF])

    def mkseq(name, T, past, s, x, y, ckvo, kpeo, glao, convo):
        TK = past + T
        ntk = (TK + 127) // 128
        return dict(name=name, T=T, past=past, s=s, x=x, y=y, ckvo=ckvo, kpeo=kpeo, glao=glao, convo=convo,
                    TK=TK, ntk=ntk,
                    kTs=dscr("kTs_" + name, [8, 96, ntk * 128]),
                    vs=dscr("vs_" + name, [4, 128, ntk, 192]),
                    mxs=dscr("mxs_" + name, [D, T]))

    seqs = [mkseq("s", TS, PAST, 1, I["xs"], O["ys"], O["ckvs"], O["kpes"], O["glas"], O["convs"]),
            mkseq("p", TP, 0, 0, I["xp"], O["yp"], O["ckvp"], O["kpep"], O["glap"], O["convp"])]
    wups = dscr("wups", [2 * NJ, 128, 8, 128])
    wdowns = dscr("wdowns", [FF, D])
    gbcs = dscr("gbcs", [2, 128, 2048], F32)

    semstack = contextlib.ExitStack()
    semn = [0]

    def semget(sk):
        semn[0] += 1
        return semstack.enter_context(nc.semaphore("s%d" % semn[0]))

    def sb(st, name, shape, dt=F32):
        return st.enter_context(nc.sbuf_tensor("S_" + name, list(shape), dt))

    def ring(st, name, shape, dt, n):
        return Ring([sb(st, "%s%d" % (name, i), shape, dt) for i in range(n)], name)

    def dma(eng, out, in_, key, r=(), w=()):
        def fn(e, sem):
            e.dma_start(out=out, in_=in_).then_inc(sem, 16)
        return P.add(eng, fn, r=r, w=w, dma=key)

    top = contextlib.ExitStack()
    with semstack, top:
        banks = [top.enter_context(nc.psum_tensor("psb%d" % i, [128, 512], F32)) for i in range(8)]
        PS = PsumAlloc(banks)
        identf = sb(top, "identf", [128, 128])
        identb = sb(top, "identb", [128, 128], BF16)
        trif = sb(top, "trif", [128, 128])
        onesf = sb(top, "onesf", [128, 128])
        mods = sb(top, "mods", [128, 48, 2])
        mul1 = sb(top, "mul1", [128, 2, 8])
        mul2 = sb(top, "mul2", [128, 2, 8])
        g1fm = sb(top, "g1fm", [128, 8])
        g2fm = sb(top, "g2fm", [128, 8])
        dma("sp", identf[:], I["ident"], "c_identf", w=["identf"])
        dma("sp", trif[:], I["tri"], "c_trif", w=["trif"])
        dma("sp", onesf[:], I["ones"], "c_onesf", w=["onesf"])
        dma("sp", g1fm[:], I["g1_fm"], "c_g1", w=["g1fm"])
        dma("sp", g2fm[:], I["g2_fm"], "c_g2", w=["g2fm"])
        P.add("dve", lambda e: e.tensor_copy(out=identb[:], in_=identf[:]), r=["identf"], w=["identb"])

        with contextlib.ExitStack() as st:
            cTf = sb(st, "cTf", [128, 8, 2])
            scT = sb(st, "scT", [128, 8, 2], BF16)
            acc = sb(st, "acc", [128, 96])
            badaf = sb(st, "badaf", [128, 48])
            bada2 = sb(st, "bada2", [2, 2048])
            gates = sb(st, "gates", [2, 2048])
            sel2 = sb(st, "sel2", [2, 2, 128])
            gst = sb(st, "gst", [128, 2048])
            war = ring(st, "wa", [128, 6 * D], BF16, 2)
            for c in range(8):
                dma("pool", wups.rearrange("j p c n -> p c j n")[:, c, :, :],
                    I["wup"][c * 128:(c + 1) * 128, :].rearrange("p (j n) -> p j n", n=128),
                    "wups%d" % c, w=[("wups", c)])
            dma("pool", wdowns, I["wdown"], "wdowns", w=["wdowns"])
            dma("sp", cTf[:], I["cT"], "c_cT", w=["cTf"])
            dma("sp", badaf[:], I["bada_fm"], "c_badaf", w=["badaf"])
            dma("sp", bada2[:], I["bada2"], "c_bada2", w=["bada2"])
            dma("sp", sel2[:], I["sel2"].rearrange("s k m -> k s m"), "c_sel2", w=["sel2"])
            P.add("act", lambda e: e.activation(out=scT[:], in_=cTf[:], func=AF.Silu), r=["cTf"], w=["scT"])
            P.add("dve", lambda e: e.memset(acc[:], 0.0), w=["acc"])
            gb = [PS.get() for _ in range(4)]
            for c in range(8):
                wa, kwa = war.next()
                dma("pool", wa[:], I["wada"][c * 128:(c + 1) * 128, :], "wa%d" % (c % 2), w=[kwa])
                pb, kb, ib = PS.get()

                def fm(e, wa=wa, pb=pb, c=c):
                    for m in range(48):
                        ins = e.matmul(pb[:, 2 * m:2 * m + 2], lhsT=wa[:, m * 128:(m + 1) * 128], rhs=scT[:, c, :],
                                       start=True, stop=True)
                    return ins
                P.add("pe", fm, r=[kwa, "scT"], w=[kb])
                P.add("dve", lambda e, pb=pb: e.tensor_tensor(out=acc[:], in0=pb[:, 0:96], in1=acc[:], op=ALU.add),
                      r=[], w=[kb, "acc"])
                PS.rel(ib)
                for gi in range(4):
                    col = (2048 if gi < 2 else 5120) + (gi % 2) * 512
                    P.add("pe", lambda e, wa=wa, c=c, gi=gi, col=col: e.matmul(
                        gb[gi][0][0:2, :], lhsT=scT[:, c, :], rhs=wa[:, col:col + 512], start=(c == 0), stop=(c == 7)),
                        r=[kwa, "scT"], w=[gb[gi][1]])
            P.add("dve", lambda e: e.tensor_tensor(
                out=mods[:], in0=acc[:].rearrange("p (m s) -> p m s", s=2),
                in1=badaf[:].unsqueeze(2).broadcast_to([128, 48, 2]), op=ALU.add), r=["acc", "badaf"], w=["mods"])
            for s in range(2):
                for (dst, gfm, sc) in ((mul1, g1fm, 1), (mul2, g2fm, 4)):
                    P.add("dve", lambda e, dst=dst, gfm=gfm, sc=sc, s=s: e.scalar_tensor_tensor(
                        out=dst[:, s, :], in0=mods[:, sc * 8:(sc + 1) * 8, s], scalar=1.0, in1=gfm[:],
                        op0=ALU.add, op1=ALU.mult), r=["mods", "g1fm", "g2fm"], w=[("mul", id(dst), s)])
            for gi in range(4):
                P.add("dve", lambda e, gi=gi: e.tensor_tensor(
                    out=gates[:, gi * 512:(gi + 1) * 512], in0=gb[gi][0][0:2, :], in1=bada2[:, gi * 512:(gi + 1) * 512],
                    op=ALU.add), r=["bada2"], w=[gb[gi][1], "gates"])
                PS.rel(gb[gi][2])
            for s in range(2):
                for gi in range(4):
                    pb, kb, ib = PS.get()
                    P.add("pe", lambda e, pb=pb, s=s, gi=gi: e.matmul(
                        pb[:, :], lhsT=sel2[:, s, :], rhs=gates[:, gi * 512:(gi + 1) * 512], start=True, stop=True),
                        r=["sel2", "gates"], w=[kb])
                    P.add("act", lambda e, pb=pb, gi=gi: e.activation(
                        out=gst[:, gi * 512:(gi + 1) * 512], in_=pb[:, :], func=AF.Copy), w=[kb, "gst"])
                    PS.rel(ib)
                dma("sp", gbcs[s], gst[:], "gbcs_w", r=["gst"], w=[("gbcs", s)])
            P.emit(nc, semget)

        with contextlib.ExitStack() as st:
            PS_full = PsumAlloc(banks, 0)
            PS_a5 = PsumAlloc(banks[0:5], 0)
            PS_b3 = PsumAlloc(banks[5:8], 5)
            CFG = {"A": PS_full, "B": PS_full, "LA": 5}

            def set_mode(split):
                if split:
                    CFG.update(A=PS_a5, B=PS_b3, LA=2)
                else:
                    CFG.update(A=PS_full, B=PS_full, LA=5)
            winb = sb(st, "winb", [128, 8, 2224], BF16)
            wuqb = sb(st, "wuqb", [128, 3, 768], BF16)
            wukvb = sb(st, "wukvb", [128, 2, 1024], BF16)
            wa2b = sb(st, "wa2b", [16, 256], BF16)
            gqafm = sb(st, "gqafm", [128, 3])
            gqnbc = sb(st, "gqnbc", [128, 768])
            gkvabc = sb(st, "gkvabc", [128, 256])
            gknbc = sb(st, "gknbc", [128, 96])
            ba2bc = sb(st, "ba2bc", [128, 256])
            dma("pool", winb[:], I["win"].rearrange("(c p) n -> p c n", p=128), "l_win", w=["winb"])
            dma("pool", wukvb[:], I["wukv"].rearrange("(c p) n -> p c n", p=128), "l_wukv", w=["wukvb"])
            dma("pool", wa2b[:], I["wa2"], "l_wa2", w=["wa2b"])
            dma("sp", gqafm[:], I["gqa_fm"], "l_gqa", w=["gqafm"])
            dma("sp", gqnbc[:], I["gqn8"].partition_broadcast(128), "l_gqn", w=["gqnbc"])
            dma("sp", gkvabc[:], I["gkva"].partition_broadcast(128), "l_gkva", w=["gkvabc"])
            dma("sp", gknbc[:], I["gkn"].partition_broadcast(128), "l_gkn", w=["gknbc"])
            dma("sp", ba2bc[:], I["ba2"].partition_broadcast(128), "l_ba2", w=["ba2bc"])
            P.add("dve", lambda e: e.tensor_scalar(out=gqnbc[:], in0=gqnbc[:], scalar1=96.0 ** -0.5, scalar2=None,
                                                   op0=ALU.mult), r=[], w=["gqnbc"])

            xr = ring(st, "xt", [128, D], F32, 2)
            for c in range(3):
                xt_, kx_ = xr.next()
                dma("sp", xt_[:, 0:768], I["wuq"][c * 128:(c + 1) * 128, :], "l_wuq%d" % kx_[1], w=[kx_])
                P.add("dve", lambda e, c=c, xt_=xt_: e.tensor_scalar(out=wuqb[:, c, :], in0=xt_[:, 0:768], scalar1=gqafm[:, c:c + 1],
                                                                    scalar2=None, op0=ALU.mult), r=[kx_, "gqafm"], w=["wuqb"])
            junk = sb(st, "junk", [128, D], BF16)
            xsr = ring(st, "xs_", [128, D], BF16, 2)
            hTr = ring(st, "hT", [128, 8, 128], BF16, 2)
            st8 = ring(st, "st8", [128, 64], F32, 4)
            sqf = sb(st, "sqf", [128, 768])
            qn = sb(st, "qn", [128, 768])
            qln = sb(st, "qln", [128, 384], BF16)
            qlT = sb(st, "qlT", [128, 3, 128], BF16)
            qb = sb(st, "qb", [128, 768], BF16)
            tmpa = sb(st, "tmpa", [128, 256])
            tmpb = sb(st, "tmpb", [128, 256])
            ckvr = ring(st, "ckvf", [128, 256], F32, 2)
            kper = ring(st, "kpef", [128, 32], F32, 2)
            glrb = sb(st, "glrb", [128, 16], BF16)
            ckvb = sb(st, "ckvb", [128, 256], BF16)
            ckvT = sb(st, "ckvT", [128, 2, 128], BF16)
            sqk = sb(st, "sqk", [128, 512])
            kn = sb(st, "kn", [128, 768])
            kb_ = sb(st, "kb_", [128, 768], BF16)
            kg = sb(st, "kg", [128, 32])
            kr = sb(st, "kr", [128, 32])
            vstr = ring(st, "vst", [128, 4, 192], BF16, 2)
            tabr = ring(st, "tab", [128, 64], F32, 2)
            glrT = sb(st, "glrT", [16, 128], BF16)
            zb = sb(st, "zb", [128, 256])
            ez = sb(st, "ez", [128, 256])
            spl = sb(st, "spl", [128, 256])
            eb = sb(st, "eb", [128, 256])
            enb = sb(st, "enb", [128, 256])
            gfm = sb(st, "gfm", [64, 4])
            qt = sb(st, "qt", [128, 256], BF16)
            kt = sb(st, "kt", [128, 256], BF16)
            vb = sb(st, "vb", [128, 512], BF16)
            qkT = sb(st, "qkT", [64, 8, 128], BF16)
            ATb = sb(st, "ATb", [128, 4, 128], BF16)
            sg = sb(st, "sg", [128, 512])
            tmpo = sb(st, "tmpo", [128, 512])
            gated = sb(st, "gated", [128, 512], BF16)
            Sf = sb(st, "Sf", [64, 4, 128])
            Sg_ = sb(st, "Sg_", [64, 4, 128])
            Sb = sb(st, "Sb", [64, 4, 128], BF16)
            QTr = ring(st, "QT", [96, 8, 512], BF16, 2)
            KTb = sb(st, "KTb", [96, 8, 512], BF16)
            mxr = ring(st, "mxb", [128, 8, 512], BF16, 2)
            wsr = ring(st, "wsb", [128, 2224], F32, 2)
            vchr = ring(st, "vch", [128, 16, 192], BF16, 2)
            kchr = ring(st, "kch", [96, 2048], BF16, 2)
            ptr = ring(st, "pt", [128, 512], BF16, 6)
            recr = ring(st, "rec", [128, 512], F32, 1)
            osbr = ring(st, "osb", [128, 512], F32, 2)
            for t_ in vstr.t:
                P.add("dve", lambda e, t_=t_: e.memset(t_[:, :, :], 0.0), w=[("vst", vstr.t.index(t_))])
                P.add("dve", lambda e, t_=t_: e.memset(t_[:, :, 64:128], 1.0), w=[("vst", vstr.t.index(t_))])

            def rstd_chain(src, n, scale, key, tn):
                P.add("act", lambda e: e.activation(out=src[0:tn, 0:n], in_=src[0:tn, 0:n], func=AF.Ln, scale=scale, bias=EPS),
                      r=[], w=[key])
                P.add("act", lambda e: e.activation(out=src[0:tn, 0:n], in_=src[0:tn, 0:n], func=AF.Exp, scale=-0.5),
                      r=[], w=[key])

            def transposes(srcs, tn, key_r, evac, key_w, extra_w=()):
                pb, kb, ib = CFG["B"].get()
                pbv = bf(pb[:, :])

                def fn(e):
                    for i, sap in enumerate(srcs):
                        m = sap.shape[1]
                        ins = e.transpose(out=pbv[0:m, i * tn:(i + 1) * tn], in_=sap, identity=identb[0:tn, 0:tn])
                    return ins
                P.add("pe", fn, r=list(key_r) + ["identb"], w=[kb])
                evac(pbv, kb)
                CFG["B"].rel(ib)

            def keyside(sq, ti, tn, ckvf, kckv, kpef, kkpe, tab, ktab, col):
                s8, ks8 = st8.next()
                vst, kvst = vstr.next()
                P.add("act", lambda e: e.activation(out=ckvb[0:tn, :], in_=ckvf[0:tn, :], func=AF.Copy), r=[kckv], w=["ckvb"])

                def ev(pbv, kb):
                    P.add("dve", lambda e: e.tensor_copy(out=ckvT[:, :, 0:tn], in_=pbv[:, 0:2 * tn].rearrange("p (c t) -> p c t", c=2)),
                          r=[], w=[kb, "ckvT"])
                transposes([ckvb[0:tn, c * 128:(c + 1) * 128] for c in range(2)], tn, ["ckvb"], ev, "ckvT")
                yield
                kbk = [CFG["B"].get() for _ in range(2)]
                for g in range(2):
                    def mm(e, g=g):
                        for c in range(2):
                            ins = e.matmul(kbk[g][0][0:tn, :], lhsT=ckvT[:, c, 0:tn], rhs=wukvb[:, c, g * 512:(g + 1) * 512],
                                           start=(c == 0), stop=(c == 1))
                        return ins
                    P.add("pe", mm, r=["ckvT", "wukvb"], w=[kbk[g][1]])
                for g in range(2):
                    K3 = kbk[g][0][0:tn, :].rearrange("p (h d) -> p h d", h=4)
                    P.add("act", lambda e, K3=K3, g=g: e.activation(
                        out=sqk[0:tn, g * 256:(g + 1) * 256].rearrange("p (h d) -> p h d", h=4), in_=K3[:, :, 0:64], func=AF.Square),
                        w=[kbk[g][1], "sqk"])
                    K4 = kbk[g][0][0:tn, :].rearrange("p (a b d) -> p a b d", a=2, b=2)
                    for b2 in range(2):
                        P.add("act", lambda e, K4=K4, g=g, b2=b2: e.activation(
                            out=vst[0:tn, 2 * g:2 * g + 2, b2 * 128:b2 * 128 + 64], in_=K4[:, :, b2, 64:128], func=AF.Copy),
                            w=[kbk[g][1], kvst])
                yield
                P.add("dve", lambda e: e.tensor_reduce(out=s8[0:tn, 0:8], in_=sqk[0:tn, :].rearrange("p (h d) -> p h d", h=8),
                                                        axis=AX.X, op=ALU.add), r=["sqk"], w=[ks8])
                P.add("act", lambda e: e.activation(out=junk[0:tn, 0:32], in_=kpef[0:tn, :], func=AF.Square,
                                                    accum_out=s8[0:tn, 8:9]), r=[kkpe], w=["junk", ks8])
                P.add("dve", lambda e: e.tensor_scalar(out=s8[0:tn, 0:8], in0=s8[0:tn, 0:8], scalar1=s8[0:tn, 8:9], scalar2=None,
                                                       op0=ALU.add), w=[ks8])
                rstd_chain(s8, 8, 1.0 / 96, ks8, tn)
                for g in range(2):
                    K3 = kbk[g][0][0:tn, :].rearrange("p (h d) -> p h d", h=4)
                    P.add("dve", lambda e, K3=K3, g=g: e.tensor_tensor(
                        out=kn[0:tn, :].rearrange("p (h d) -> p h d", h=8)[:, 4 * g:4 * g + 4, 0:64], in0=K3[:, :, 0:64],
                        in1=s8[0:tn, 4 * g:4 * g + 4].unsqueeze(2).broadcast_to([tn, 4, 64]), op=ALU.mult),
                        r=[ks8], w=[kbk[g][1], "kn"])
                    CFG["B"].rel(kbk[g][2])
                yield
                kn3 = kn[0:tn, :].rearrange("p (h d) -> p h d", h=8)
                kb3 = kb_[0:tn, :].rearrange("p (h d) -> p h d", h=8)
                P.add("dve", lambda e: e.tensor_tensor(out=kb3[:, :, 0:64], in0=kn3[:, :, 0:64],
                                                       in1=gknbc[0:tn, 0:64].unsqueeze(1).broadcast_to([tn, 8, 64]), op=ALU.mult),
                      r=["kn", "gknbc"], w=["kb_"])
                P.add("dve", lambda e: e.tensor_tensor(out=kg[0:tn, :], in0=kpef[0:tn, :], in1=gknbc[0:tn, 64:96], op=ALU.mult),
                      r=[kkpe, "gknbc"], w=["kg"])
                P.add("dve", lambda e: e.tensor_tensor(out=kr[0:tn, :], in0=kg[0:tn, :], in1=tab[0:tn, 0:32], op=ALU.mult),
                      r=["kg", ktab], w=["kr"])
                P.add("dve", lambda e: e.tensor_tensor(out=tmpa[0:tn, 0:16], in0=kg[0:tn, 16:32], in1=tab[0:tn, 32:48], op=ALU.mult),
                      r=["kg", ktab], w=["tmpa"])
                P.add("dve", lambda e: e.tensor_tensor(out=tmpa[0:tn, 16:32], in0=kg[0:tn, 0:16], in1=tab[0:tn, 48:64], op=ALU.mult),
                      r=["kg", ktab], w=["tmpa"])
                P.add("dve", lambda e: e.tensor_tensor(out=kr[0:tn, :], in0=kr[0:tn, :], in1=tmpa[0:tn, 0:32], op=ALU.add),
                      r=["tmpa"], w=["kr"])
                P.add("dve", lambda e: e.tensor_tensor(out=kb3[:, :, 64:96], in0=kr[0:tn, :].unsqueeze(1).broadcast_to([tn, 8, 32]),
                                                       in1=s8[0:tn, 0:8].unsqueeze(2).broadcast_to([tn, 8, 32]), op=ALU.mult),
                      r=["kr", ks8], w=["kb_"])

                yield

                def ev2(pbv, kb):
                    P.add("dve", lambda e: e.tensor_copy(out=KTb[:, :, col:col + tn],
                                                         in_=pbv[0:96, 0:8 * tn].rearrange("p (h t) -> p h t", h=8)),
                          r=[], w=[kb, "KTb"])
                transposes([kb_[0:tn, h * 96:(h + 1) * 96] for h in range(8)], tn, ["kb_"], ev2, "KTb")
                dma("sp", sq["vs"][:, :, ti, :].rearrange("a p d -> p a d"), vst[:, :, :], "vsw%d" % kvst[1],
                    r=[kvst], w=[("vs", sq["name"], ti)])

            def attention(sq, q0, NQ, QT, kQT, mxb, kmx, bg=None, nyield=120):
                qa = sq["past"] + q0
                tlast = (qa + NQ - 1) // 128
                nopp = 8 * (tlast + 1) + 12
                tick = [0, 0]
                kpump = 1

                def pump(k):
                    if bg is None:
                        return
                    tick[0] += k
                    want = (tick[0] * nyield) // nopp
                    while tick[1] < want:
                        tick[1] += 1
                        try:
                            next(bg)
                        except StopIteration:
                            return
                def _pair(pp):
                    bo = [CFG["A"].get() for _ in range(2)]
                    items = []
                    for kc in range(0, tlast + 1, 16):
                        nt = min(16, tlast + 1 - kc)
                        for hh in range(2):
                            for tl in range(nt):
                                items.append(dict(kc=kc, nt=nt, hh=hh, tl=tl))
                    cur = {}

                    def issue_S(it):
                        kc, nt, hh, tl = it["kc"], it["nt"], it["hh"], it["tl"]
                        h = 2 * pp + hh
                        if hh == 0 and tl == 0:
                            vch, kvch = vchr.next()
                            dma("sp", vch[:, 0:nt, :], sq["vs"][pp, :, kc:kc + nt, :], "vch%d" % kvch[1],
                                r=[("vs", sq["name"], t) for t in range(kc, kc + nt)], w=[kvch])
                            cur["v"] = (vch, kvch)
                        if tl == 0:
                            kch, kkch = kchr.next()
                            ncol = min((kc + nt) * 128, sq["TK"]) - kc * 128
                            dma("sp", kch[:, 0:ncol], sq["kTs"][h, :, kc * 128:kc * 128 + ncol], "kch%d" % kkch[1],
                                r=[("kT", sq["name"], t) for t in range(kc, kc + nt)], w=[kkch])
                            cur["k"] = (kch, kkch)
                        kch, kkch = cur["k"]
                        it["v"] = cur["v"]
                        t = kc + tl
                        tk = min(128, sq["TK"] - 128 * t)
                        jlo = 128 * t - qa
                        qlo = max(0, jlo)
                        pb, kb, ib = CFG["A"].get()
                        it.update(t=t, tk=tk, jlo=jlo, qlo=qlo, pb=pb, kb=kb, ib=ib)
                        P.add("pe", lambda e: e.matmul(
                            pb[0:tk, qlo:NQ], lhsT=kch[:, tl * 128:tl * 128 + tk], rhs=QT[:, h, qlo:NQ], start=True, stop=True),
                            r=[kkch, kQT], w=[kb])

                    def issue_rest(it):
                        hh, tl, t, tk, jlo, qlo, pb, kb, ib = (it[k] for k in ("hh", "tl", "t", "tk", "jlo", "qlo", "pb", "kb", "ib"))
                        vch, kvch = it["v"]
                        pt, kpt = ptr.next()
                        P.add("act", lambda e: e.activation(out=pt[0:tk, qlo:NQ], in_=pb[0:tk, qlo:NQ], func=AF.Exp), w=[kb, kpt])
                        CFG["A"].rel(ib)
                        if jlo >= 0 and tk == 128:
                            P.add("dve", lambda e: e.memset(pt[64:128, qlo:qlo + 64], 0.0), w=[kpt])
                        P.add("pe", lambda e: e.matmul(
                            bo[hh][0][:, qlo:NQ], lhsT=vch[0:tk, tl, hh * 64:hh * 64 + 128], rhs=pt[0:tk, qlo:NQ],
                            start=(t == 0), stop=(t == tlast)), r=[kpt, kvch], w=[bo[hh][1]])
                    LA = CFG["LA"]
                    for i in range(len(items) + LA):
                        if i < len(items):
                            issue_S(items[i])
                        if i >= LA:
                            issue_rest(items[i - LA])
                        pump(kpump)
                    for hh in range(2):
                        rec, krec = recr.next()
                        osb, kosb = osbr.next()
                        so, oo = ((64, 0), (0, 64))[hh]
                        P.add("dve", lambda e, osb=osb, hh=hh: e.tensor_copy(out=osb[:, 0:NQ], in_=bo[hh][0][:, 0:NQ]), w=[bo[hh][1], kosb])
                        CFG["A"].rel(bo[hh][2])
                        P.add("dve", lambda e, rec=rec, osb=osb, so=so, oo=oo: e.reciprocal(out=rec[oo:oo + 64, 0:NQ], in_=osb[so:so + 64, 0:NQ]),
                              r=[kosb], w=[krec])
                        P.add("dve", lambda e, rec=rec, osb=osb, oo=oo, pp=pp: e.tensor_tensor(
                            out=mxb[oo:oo + 64, pp, 0:NQ], in0=osb[oo:oo + 64, 0:NQ], in1=rec[oo:oo + 64, 0:NQ], op=ALU.mult),
                            r=[krec, kosb], w=[kmx])

                for pp in range(4):
                    _pair(pp)
                pump(100000)

            def _p1seq(sq):
                s = sq["s"]
                T, past = sq["T"], sq["past"]
                if past:
                    dma("sp", Sf[:, :, :], I["sgla"].rearrange("h k v -> k h v"), "l_S", w=["Sf"])
                    P.add("act", lambda e: e.activation(out=Sb[:], in_=Sf[:], func=AF.Copy), r=["Sf"], w=["Sb"])
                else:
                    P.add("dve", lambda e: e.memset(Sf[:], 0.0), w=["Sf"])
                    P.add("dve", lambda e: e.memset(Sb[:], 0.0), w=["Sb"])
                def _ctile(ti):
                    col = (ti % 4) * 128
                    ckvf, kckv = ckvr.next()
                    kpef, kkpe = kper.next()
                    tab, ktab = tabr.next()
                    dma("sp", ckvf[:], I["cckv"][ti * 128:(ti + 1) * 128, :], "l_ckv%d" % kckv[1], w=[kckv])
                    dma("sp", kpef[:], I["ckpe"][ti * 128:(ti + 1) * 128, :], "l_kpe%d" % kkpe[1], w=[kkpe])
                    dma("sp", tab[:], I["tab"][ti * 128:(ti + 1) * 128, :], "l_tab%d" % ktab[1], w=[ktab])
                    yield from keyside(sq, ti, 128, ckvf, kckv, kpef, kkpe, tab, ktab, col)
                for t4 in range(0, past // 128, 4):
                    pend = list(range(t4, t4 + 4))
                    run = []
                    since = 10 ** 9
                    while pend or run:
                        if pend and len(run) < 2 and since >= 3:
                            run.append(_ctile(pend.pop(0)))
                            since = 0
                        for g_ in list(run):
                            try:
                                next(g_)
                            except StopIteration:
                                run.remove(g_)
                        since += 1
                    dma("sp", sq["kTs"][:, :, t4 * 128:(t4 + 4) * 128].rearrange("h d t -> d h t"), KTb[:, :, :], "kTw",
                        r=["KTb"], w=[("kT", sq["name"], t) for t in range(t4, t4 + 4)])
                NQB = min(512, T)
                ntile = (NQB + 127) // 128

                def _p1blk(q0, info, gen=False):
                    QT, kQT = QTr.next()
                    mxb, kmx = mxr.next()
                    info.update(QT=QT, kQT=kQT, mxb=mxb, kmx=kmx)

                    def _p1tile(it):
                        t0 = q0 + it * 128
                        tn = min(128, T - t0)
                        col = it * 128
                        ti = (past + t0) // 128
                        xt, kx = xr.next()
                        xs_, kxs = xsr.next()
                        hT, khT = hTr.next()
                        s8, ks8 = st8.next()
                        tab, ktab = tabr.next()
                        ckvf, kckv = ckvr.next()
                        kpef, kkpe = kper.next()
                        wsb, kws = wsr.next()
                        dma(XQ, xt[0:tn, :], sq["x"][t0:t0 + tn, :], "l_x%d" % kx[1], w=[kx])
                        dma(XQ, tab[0:tn, :], I["tab"][past + t0:past + t0 + tn, :], "l_tab%d" % ktab[1], w=[ktab])
                        P.add("act", lambda e, xt=xt, s8=s8, tn=tn: e.activation(out=junk[0:tn, :], in_=xt[0:tn, :], func=AF.Square,
                                                                                 accum_out=s8[0:tn, 16:17]), r=[kx], w=["junk", ks8])
                        rstd_chain(s8[:, 16:17], 1, 1.0 / D, ks8, tn)
                        P.add("act", lambda e, xt=xt, xs_=xs_, s8=s8, tn=tn: e.activation(
                            out=xs_[0:tn, :], in_=xt[0:tn, :], func=AF.Identity, scale=s8[0:tn, 16:17]), r=[kx, ks8], w=[kxs])

                        def ev_h(pbv, kb, hT=hT, khT=khT, tn=tn, s=s):
                            def fn(e):
                                for c in range(8):
                                    ins = e.tensor_scalar(out=hT[:, c, 0:tn], in0=pbv[:, c * tn:(c + 1) * tn], scalar1=mul1[:, s, c:c + 1],
                                                          scalar2=mods[:, c, s:s + 1], op0=ALU.mult, op1=ALU.add)
                                return ins
                            P.add("dve", fn, r=["mods", ("mul", id(mul1), s)], w=[kb, khT])
                        transposes([xs_[0:tn, c * 128:(c + 1) * 128] for c in range(8)], tn, [kxs], ev_h, khT)
                        yield
                        groups = [(0, 384), (384, 288), (672, 512), (1184, 512), (1712, 512)]
                        for gi, (c0, n) in enumerate(groups):
                            gb_, gk_, gi_ = CFG["B"].get()

                            def mm(e, gb_=gb_, c0=c0, n=n):
                                for c in range(8):
                                    ins = e.matmul(gb_[0:tn, 0:n], lhsT=hT[:, c, 0:tn], rhs=winb[:, c, c0:c0 + n],
                                                   start=(c == 0), stop=(c == 7))
                                return ins
                            P.add("pe", mm, r=[khT, "winb"], w=[gk_])
                            if gi == 1:
                                def mm2(e, gb_=gb_):
                                    for c in range(8):
                                        ins = e.matmul(gb_[0:tn, 288:304], lhsT=hT[:, c, 0:tn], rhs=winb[:, c, 1696:1712],
                                                       start=(c == 0), stop=(c == 7))
                                    return ins
                                P.add("pe", mm2, r=[khT, "winb"], w=[gk_])
                                P.add("dve", lambda e, gb_=gb_: e.tensor_copy(out=wsb[0:tn, 384:672], in_=gb_[0:tn, 0:288]), w=[gk_, kws])
                                P.add("dve", lambda e, gb_=gb_: e.tensor_copy(out=wsb[0:tn, 1696:1712], in_=gb_[0:tn, 288:304]), w=[gk_, kws])
                            elif gi == 0:
                                P.add("act", lambda e, gb_=gb_: e.activation(out=wsb[0:tn, 0:384], in_=gb_[0:tn, 0:384], func=AF.Copy), w=[gk_, kws])
                            elif gi == 2:
                                P.add("dve", lambda e, gb_=gb_: e.tensor_copy(out=wsb[0:tn, 672:1184], in_=gb_[0:tn, 0:512]), w=[gk_, kws])
                            elif gi == 3:
                                P.add("act", lambda e, gb_=gb_: e.activation(out=wsb[0:tn, 1184:1696], in_=gb_[0:tn, :], func=AF.Copy), w=[gk_, kws])
                            else:
                                P.add("dve", lambda e, gb_=gb_: e.tensor_copy(out=wsb[0:tn, 1712:2224], in_=gb_[0:tn, :]), w=[gk_, kws])
                            CFG["B"].rel(gi_)
                            yield
                        P.add("act", lambda e, tn=tn, s8=s8: e.activation(out=junk[0:tn, 0:256], in_=wsb[0:tn, 384:640], func=AF.Square,
                                                                          accum_out=s8[0:tn, 17:18]), r=[kws], w=["junk", ks8])
                        rstd_chain(s8[:, 17:18], 1, 1.0 / 256, ks8, tn)
                        P.add("dve", lambda e, tn=tn, s8=s8, ckvf=ckvf: e.scalar_tensor_tensor(
                            out=ckvf[0:tn, :], in0=wsb[0:tn, 384:640], scalar=s8[0:tn, 17:18], in1=gkvabc[0:tn, :],
                            op0=ALU.mult, op1=ALU.mult), r=[ks8, "gkvabc", kws], w=[kckv])
                        P.add("act", lambda e, tn=tn, kpef=kpef: e.activation(out=kpef[0:tn, :], in_=wsb[0:tn, 640:672], func=AF.Copy),
                              r=[kws], w=[kkpe])
                        yield
                        dma("sp", sq["ckvo"][t0:t0 + tn, :], ckvf[0:tn, :], "o_ckv%d" % kckv[1], r=[kckv])
                        dma("sp", sq["kpeo"][t0:t0 + tn, :], kpef[0:tn, :], "o_kpe%d" % kkpe[1], r=[kkpe])
                        ks8q = (ks8, "q")
                        ks8g = (ks8, "g")

                        def _qgen():
                            P.add("act", lambda e, tn=tn, s8=s8: e.activation(out=junk[0:tn, 0:384], in_=wsb[0:tn, 0:384], func=AF.Square,
                                                                              accum_out=s8[0:tn, 18:19]), r=[kws], w=["junk", ks8q])
                            rstd_chain(s8[:, 18:19], 1, 1.0 / 384, ks8q, tn)
                            P.add("act", lambda e, tn=tn, s8=s8: e.activation(out=qln[0:tn, :], in_=wsb[0:tn, 0:384], func=AF.Identity,
                                                                              scale=s8[0:tn, 18:19]), r=[ks8q, kws], w=["qln"])
                            yield

                            def ev_q(pbv, kb, tn=tn):
                                P.add("dve", lambda e: e.tensor_copy(out=qlT[:, :, 0:tn], in_=pbv[:, 0:3 * tn].rearrange("p (c t) -> p c t", c=3)),
                                      w=[kb, "qlT"])
                            transposes([qln[0:tn, c * 128:(c + 1) * 128] for c in range(3)], tn, ["qln"], ev_q, "qlT")
                            yield
                            qbk = [CFG["B"].get() for _ in range(2)]
                            for g, (c0, n) in enumerate(((0, 480), (480, 288))):
                                def mmq(e, g=g, c0=c0, n=n, tn=tn):
                                    for c in range(3):
                                        ins = e.matmul(qbk[g][0][0:tn, 0:n], lhsT=qlT[:, c, 0:tn], rhs=wuqb[:, c, c0:c0 + n],
                                                       start=(c == 0), stop=(c == 2))
                                    return ins
                                P.add("pe", mmq, r=["qlT", "wuqb"], w=[qbk[g][1]])
                            for g, (c0, n) in enumerate(((0, 480), (480, 288))):
                                P.add("act", lambda e, g=g, c0=c0, n=n, tn=tn: e.activation(out=sqf[0:tn, c0:c0 + n], in_=qbk[g][0][0:tn, 0:n],
                                                                                            func=AF.Square), w=[qbk[g][1], "sqf"])
                            P.add("dve", lambda e, tn=tn, s8=s8: e.tensor_reduce(out=s8[0:tn, 24:32], in_=sqf[0:tn, :].rearrange("p (h d) -> p h d", h=8),
                                                                                 axis=AX.X, op=ALU.add), r=["sqf"], w=[ks8q])
                            yield
                            rstd_chain(s8[:, 24:32], 8, 1.0 / 96, ks8q, tn)
                            for g, (h0, nh) in enumerate(((0, 5), (5, 3))):
                                P.add("dve", lambda e, g=g, h0=h0, nh=nh, tn=tn, s8=s8: e.tensor_tensor(
                                    out=qn[0:tn, h0 * 96:(h0 + nh) * 96].rearrange("p (h d) -> p h d", h=nh),
                                    in0=qbk[g][0][0:tn, 0:nh * 96].rearrange("p (h d) -> p h d", h=nh),
                                    in1=s8[0:tn, 24 + h0:24 + h0 + nh].unsqueeze(2).broadcast_to([tn, nh, 96]), op=ALU.mult),
                                    r=[ks8q], w=[qbk[g][1], "qn"])
                                CFG["B"].rel(qbk[g][2])
                            P.add("dve", lambda e, tn=tn: e.tensor_tensor(out=qn[0:tn, :], in0=qn[0:tn, :], in1=gqnbc[0:tn, :], op=ALU.mult),
                                  r=["gqnbc"], w=["qn"])
                            yield
                            qn3 = qn[0:tn, :].rearrange("p (h d) -> p h d", h=8)
                            qb3 = qb[0:tn, :].rearrange("p (h d) -> p h d", h=8)
                            ta3 = tmpa[0:tn, :].rearrange("p (h d) -> p h d", h=8)
                            tb3 = tmpb[0:tn, :].rearrange("p (h d) -> p h d", h=8)
                            P.add("dve", lambda e, qn3=qn3, qb3=qb3: e.tensor_copy(out=qb3[:, :, 0:64], in_=qn3[:, :, 0:64]),
                                  r=["qn"], w=["qb"])
                            P.add("dve", lambda e, qn3=qn3, ta3=ta3, tab=tab, tn=tn: e.tensor_tensor(
                                out=ta3, in0=qn3[:, :, 64:96], in1=tab[0:tn, 0:32].unsqueeze(1).broadcast_to([tn, 8, 32]), op=ALU.mult),
                                r=["qn", ktab], w=["tmpa"])
                            P.add("dve", lambda e, qn3=qn3, tb3=tb3, tab=tab, tn=tn: e.tensor_tensor(
                                out=tb3[:, :, 0:16], in0=qn3[:, :, 80:96], in1=tab[0:tn, 32:48].unsqueeze(1).broadcast_to([tn, 8, 16]), op=ALU.mult),
                                r=["qn", ktab], w=["tmpb"])
                            P.add("dve", lambda e, qn3=qn3, tb3=tb3, tab=tab, tn=tn: e.tensor_tensor(
                                out=tb3[:, :, 16:32], in0=qn3[:, :, 64:80], in1=tab[0:tn, 48:64].unsqueeze(1).broadcast_to([tn, 8, 16]), op=ALU.mult),
                                r=["qn", ktab], w=["tmpb"])
                            P.add("dve", lambda e, qb3=qb3, ta3=ta3, tb3=tb3: e.tensor_tensor(out=qb3[:, :, 64:96], in0=ta3, in1=tb3, op=ALU.add),
                                  r=["tmpa", "tmpb"], w=["qb"])

                            yield

                            def ev_Q(pbv, kb, tn=tn, QT=QT, col=col):
                                P.add("dve", lambda e: e.tensor_copy(out=QT[:, :, col:col + tn],
                                                                     in_=pbv[0:96, 0:8 * tn].rearrange("p (h t) -> p h t", h=8)),
                                      w=[kb, kQT])
                            transposes([qb[0:tn, h * 96:(h + 1) * 96] for h in range(8)], tn, ["qb"], ev_Q, kQT)

                        def _ggen():
                            P.add("dve", lambda e: e.tensor_copy(out=glrb[0:tn, :], in_=wsb[0:tn, 1696:1712]), r=[kws], w=["glrb"])

                            def ev_g(pbv, kb, tn=tn):
                                P.add("dve", lambda e: e.tensor_copy(out=glrT[:, 0:tn], in_=pbv[0:16, 0:tn]), w=[kb, "glrT"])
                            transposes([glrb[0:tn, :]], tn, ["glrb"], ev_g, "glrT")
                            pz, kz, iz = CFG["B"].get()
                            P.add("pe", lambda e, pz=pz, tn=tn: e.matmul(pz[0:tn, 0:256], lhsT=glrT[:, 0:tn], rhs=wa2b[:, :], start=True, stop=True),
                                  r=["glrT", "wa2b"], w=[kz])
                            P.add("dve", lambda e, pz=pz, tn=tn: e.tensor_tensor(out=zb[0:tn, :], in0=pz[0:tn, 0:256], in1=ba2bc[0:tn, :], op=ALU.add),
                                  r=["ba2bc"], w=[kz, "zb"])
                            CFG["B"].rel(iz)
                            P.add("act", lambda e, tn=tn: e.activation(out=ez[0:tn, :], in_=zb[0:tn, :], func=AF.Exp, scale=-1.0), r=["zb"], w=["ez"])
                            P.add("act", lambda e, tn=tn: e.activation(out=spl[0:tn, :], in_=ez[0:tn, :], func=AF.Ln, bias=1.0), r=["ez"], w=["spl"])
                            yield
                            pc, kc_, ic = CFG["B"].get()
                            P.add("pe", lambda e, pc=pc, tn=tn: e.matmul(pc[0:tn, 0:256], lhsT=trif[0:tn, 0:tn], rhs=spl[0:tn, :], start=True, stop=True),
                                  r=["spl", "trif"], w=[kc_])

                            def mmt(e, pc=pc, tn=tn):
                                for h in range(4):
                                    ins = e.matmul(pc[0:64, 256 + 2 * h:256 + 2 * h + 2], lhsT=spl[0:tn, h * 64:(h + 1) * 64], rhs=onesf[0:tn, 0:2],
                                                   start=True, stop=True)
                                return ins
                            P.add("pe", mmt, r=["spl", "onesf"], w=[kc_])
                            P.add("act", lambda e, pc=pc, tn=tn: e.activation(out=eb[0:tn, :], in_=pc[0:tn, 0:256], func=AF.Exp, scale=-1.0 / 16,
                                                                              bias=math.log(0.125)), w=[kc_, "eb"])
                            P.add("act", lambda e, pc=pc, tn=tn: e.activation(out=enb[0:tn, :], in_=pc[0:tn, 0:256], func=AF.Exp, scale=1.0 / 16),
                                  w=[kc_, "enb"])
                            P.add("act", lambda e, pc=pc: e.activation(out=gfm[:, :], in_=pc[0:64, 256:264].rearrange("p (a b) -> p a b", b=2)[:, :, 0],
                                                                       func=AF.Exp, scale=-1.0 / 16), w=[kc_, "gfm"])
                            CFG["B"].rel(ic)
                            P.add("dve", lambda e, tn=tn: e.tensor_tensor(out=qt[0:tn, :], in0=wsb[0:tn, 672:928], in1=eb[0:tn, :], op=ALU.mult),
                                  r=["eb", kws], w=["qt"])
                            P.add("dve", lambda e, tn=tn: e.tensor_tensor(out=kt[0:tn, :], in0=wsb[0:tn, 928:1184], in1=enb[0:tn, :], op=ALU.mult),
                                  r=["enb", kws], w=["kt"])
                            P.add("act", lambda e: e.activation(out=vb[0:tn, :], in_=wsb[0:tn, 1184:1696], func=AF.Copy), r=[kws], w=["vb"])
                            yield

                            yield

                            def ev_qk(pbv, kb, tn=tn):
                                P.add("dve", lambda e: e.tensor_copy(out=qkT[:, :, 0:tn], in_=pbv[0:64, 0:8 * tn].rearrange("p (c t) -> p c t", c=8)),
                                      w=[kb, "qkT"])
                            transposes([qt[0:tn, h * 64:(h + 1) * 64] for h in range(4)] + [kt[0:tn, h * 64:(h + 1) * 64] for h in range(4)],
                                       tn, ["qt", "kt"], ev_qk, "qkT")
                            yield
                            pa, ka, ia = CFG["B"].get()

                            def mma(e, pa=pa, tn=tn):
                                for h in range(4):
                                    ins = e.matmul(pa[0:tn, h * tn:(h + 1) * tn], lhsT=qkT[:, 4 + h, 0:tn], rhs=qkT[:, h, 0:tn],
                                                   start=True, stop=True)
                                return ins
                            P.add("pe", mma, r=["qkT"], w=[ka])
                            P.add("dve", lambda e, pa=pa, tn=tn: e.tensor_tensor(
                                out=ATb[0:tn, :, 0:tn], in0=pa[0:tn, 0:4 * tn].rearrange("p (h t) -> p h t", h=4),
                                in1=trif[0:tn, 0:tn].unsqueeze(1).broadcast_to([tn, 4, tn]), op=ALU.mult), r=["trif"], w=[ka, "ATb"])
                            CFG["B"].rel(ia)
                            yield
                            po, ko, io = CFG["B"].get()

                            def mmo(e, po=po, tn=tn):
                                for h in range(4):
                                    e.matmul(po[0:tn, h * 128:(h + 1) * 128], lhsT=ATb[0:tn, h, 0:tn], rhs=vb[0:tn, h * 128:(h + 1) * 128],
                                             start=True, stop=False)
                                    ins = e.matmul(po[0:tn, h * 128:(h + 1) * 128], lhsT=qkT[:, h, 0:tn], rhs=Sb[:, h, :],
                                                   start=False, stop=True)
                                return ins
                            P.add("pe", mmo, r=["ATb", "vb", "qkT", "Sb"], w=[ko])
                            pu, ku, iu = CFG["B"].get()

                            def mmu(e, pu=pu, tn=tn):
                                for h in range(4):
                                    ins = e.matmul(pu[0:64, h * 128:(h + 1) * 128], lhsT=kt[0:tn, h * 64:(h + 1) * 64],
                                                   rhs=vb[0:tn, h * 128:(h + 1) * 128], start=True, stop=True)
                                return ins
                            P.add("pe", mmu, r=["kt", "vb"], w=[ku])

                            yield

                            def upd(e, pu=pu):
                                for pp in range(4):
                                    ins = e.tensor_scalar(out=Sg_[:, pp, :], in0=Sf[:, pp, :], scalar1=gfm[:, pp:pp + 1], scalar2=None, op0=ALU.mult)
                                return ins
                            P.add("dve", upd, r=["Sf", "gfm"], w=["Sg_"])

                            def upd2(e, pu=pu):
                                for pp in range(4):
                                    ins = e.scalar_tensor_tensor(out=Sf[:, pp, :], in0=pu[0:64, pp * 128:(pp + 1) * 128], scalar=gfm[:, pp:pp + 1],
                                                                 in1=Sg_[:, pp, :], op0=ALU.mult, op1=ALU.add)
                                return ins
                            P.add("dve", upd2, r=["Sg_", "gfm"], w=[ku, "Sf"])
                            CFG["B"].rel(iu)
                            P.add("act", lambda e: e.activation(out=Sb[:], in_=Sf[:], func=AF.Copy), r=["Sf"], w=["Sb"])

                            yield

                            def sqo(e, po=po, tn=tn, s8=s8):
                                for h in range(4):
                                    ins = e.activation(out=junk[0:tn, h * 128:(h + 1) * 128], in_=po[0:tn, h * 128:(h + 1) * 128], func=AF.Square,
                                                       accum_out=s8[0:tn, 32 + h:33 + h])
                                return ins
                            P.add("act", sqo, w=[ko, "junk", ks8g])
                            rstd_chain(s8[:, 32:36], 4, 1.0 / 128, ks8g, tn)
                            yield
                            P.add("dve", lambda e, po=po, tn=tn, s8=s8: e.tensor_tensor(
                                out=tmpo[0:tn, :].rearrange("p (h d) -> p h d", h=4), in0=po[0:tn, :].rearrange("p (h d) -> p h d", h=4),
                                in1=s8[0:tn, 32:36].unsqueeze(2).broadcast_to([tn, 4, 128]), op=ALU.mult), r=[ks8g], w=[ko, "tmpo"])
                            CFG["B"].rel(io)
                            P.add("act", lambda e: e.activation(out=sg[0:tn, :], in_=wsb[0:tn, 1712:2224], func=AF.Silu), r=[kws], w=["sg"])
                            P.add("dve", lambda e, tn=tn: e.tensor_tensor(out=gated[0:tn, :], in0=tmpo[0:tn, :], in1=sg[0:tn, :], op=ALU.mult),
                                  r=["tmpo", "sg"], w=["gated"])

                            def ev_m(pbv, kb, tn=tn, col=col):
                                P.add("dve", lambda e: e.tensor_copy(out=mxb[:, 4:8, col:col + tn],
                                                                     in_=pbv[:, 0:4 * tn].rearrange("p (c t) -> p c t", c=4)), w=[kb, kmx])
                            transposes([gated[0:tn, h * 128:(h + 1) * 128] for h in range(4)], tn, ["gated"], ev_m, kmx)
                            yield

                        def _rr(gens):
                            gens = list(gens)
                            while gens:
                                for g_ in list(gens):
                                    try:
                                        next(g_)
                                    except StopIteration:
                                        gens.remove(g_)
                                yield
                        if SUBRR:
                            yield from _rr([_qgen(), keyside(sq, ti, tn, ckvf, kckv, kpef, kkpe, tab, ktab, col), _ggen()])
                        else:
                            yield from _qgen()
                            yield from keyside(sq, ti, tn, ckvf, kckv, kpef, kkpe, tab, ktab, col)
                            yield from _ggen()
                    def spill():
                        tiA = (past + q0) // 128
                        dma("sp", sq["kTs"][:, :, tiA * 128:tiA * 128 + NQB].rearrange("h d t -> d h t"), KTb[:, :, 0:NQB], "kTw",
                            r=["KTb"], w=[("kT", sq["name"], t) for t in range(tiA, tiA + ntile)])

                    def lockstep(a, b, lag):
                        done_a = done_b = False
                        step = 0
                        while not (done_a and done_b):
                            if not done_a:
                                try:
                                    next(a)
                                except StopIteration:
                                    done_a = True
                            if (step >= lag or done_a) and not done_b:
                                try:
                                    next(b)
                                except StopIteration:
                                    done_b = True
                            step += 1

                    def _run_gen():
                        for it in range(ntile):
                            yield from _p1tile(it)
                        spill()

                    if gen:
                        return _run_gen()
                    pending = list(range(ntile))
                    running = []
                    since = 10 ** 9
                    while pending or running:
                        if pending and len(running) < 2 and since >= LAG:
                            running.append(_p1tile(pending.pop(0)))
                            since = 0
                        for g_ in list(running):
                            try:
                                next(g_)
                            except StopIteration:
                                running.remove(g_)
                        since += 1
                    spill()
                    return None

                blocks = list(range(0, T, NQB))
                infos = [dict() for _ in blocks]
                doneA = set()
                for bi, q0 in enumerate(blocks):
                    inf = infos[bi]
                    if bi not in doneA:
                        set_mode(False)
                        _p1blk(q0, inf)
                    inter = (bi >= BTH and bi + 1 < len(blocks))
                    if inter:
                        set_mode(True)
                        bg = _p1blk(blocks[bi + 1], infos[bi + 1], True)
                        attention(sq, q0, NQB, inf["QT"], inf["kQT"], inf["mxb"], inf["kmx"], bg, 108)
                        for _ in bg:
                            pass
                        doneA.add(bi + 1)
                    else:
                        set_mode(False)
                        attention(sq, q0, NQB, inf["QT"], inf["kQT"], inf["mxb"], inf["kmx"], None)
                    dma("sp", sq["mxs"].rearrange("(c p) t -> p c t", p=128)[:, :, q0:q0 + NQB], inf["mxb"][:, :, 0:NQB], "mxw%d" % inf["kmx"][1],
                        r=[inf["kmx"]], w=[("mxs", sq["name"], q0)])
                dma("sp", sq["glao"].rearrange("h k v -> k h v"), Sf[:, :, :], "o_S", r=["Sf"])
            for sq in seqs[:max(0, STAGE - 1)]:
                _p1seq(sq)
            P.emit(nc, semget)

        with contextlib.ExitStack() as st:
            woutf = sb(st, "woutf", [128, 2, D])
            woutb = sb(st, "woutb", [128, 8, D], BF16)
            gglafm = sb(st, "gglafm", [128, 1])
            wcv = sb(st, "wcv", [128, NJ, 3])
            bcv = sb(st, "bcv", [128, NJ])
            gbc = sb(st, "gbc", [128, 2048])
            hist = sb(st, "hist", [128, NJ, 2])
            mxr = ring(st, "mx", [128, 8, 512], BF16, 2)
            xr = ring(st, "x2t", [128, D], F32, 2)
            x2r = ring(st, "x2b", [128, 4, D], F32, 2)
            tmpr = ring(st, "tm", [128, 512], F32, 2)
            junk = sb(st, "junk2", [128, D], BF16)
            xsr = ring(st, "xs2", [128, D], BF16, 2)
            st8 = ring(st, "s2", [128, 8], F32, 2)
            h2r = ring(st, "h2T", [128, 8, 512], BF16, 2)
            war = ring(st, "wug", [128, 2, 8, 128], BF16, 3)
            abr = ring(st, "ab", [128, 514], F32, 2)
            cbr = ring(st, "cb", [128, 512], F32, 2)
            glr = ring(st, "gl", [128, 512], F32, 2)
            uT = sb(st, "uT", [128, NJ, 512], BF16)
            wdr = ring(st, "wd", [128, 512], BF16, 4)
            ybr = ring(st, "yb", [128, D], F32, 2)
            cvo = sb(st, "cvo", [128, 2 * NJ])
            cvt = sb(st, "cvt", [2 * NJ, 128])
            nhalf = sb(st, "nhalf", [128, 1])
            P.add("dve", lambda e: e.memset(nhalf[:], -0.5), w=["nhalf"])
            dma("sp", gglafm[:], I["ggla_fm"], "l_ggla", w=["gglafm"])
            dma("sp", wcv[:], I["wconv_fm"], "l_wcv", w=["wcv"])
            dma("sp", bcv[:], I["bconv_fm"], "l_bcv", w=["bcv"])
            for c in range(0, 8, 2):
                dma("sp", woutf[:], I["wout"][c * 128:(c + 2) * 128, :].rearrange("(c p) n -> p c n", p=128), "l_wout", r=[], w=["woutf"])
                for cc in range(2):
                    if c + cc < 4:
                        P.add("dve", lambda e, c=c, cc=cc: e.tensor_copy(out=woutb[:, c + cc, :], in_=woutf[:, cc, :]), r=["woutf"], w=["woutb"])
                    else:
                        P.add("dve", lambda e, c=c, cc=cc: e.tensor_scalar(out=woutb[:, c + cc, :], in0=woutf[:, cc, :], scalar1=gglafm[:, 0:1],
                                                                           scalar2=None, op0=ALU.mult), r=["woutf", "gglafm"], w=["woutb"])
            def _p2seq(sq):
                s = sq["s"]
                T, past = sq["T"], sq["past"]
                dma("sp", gbc[:], gbcs[s], "l_gbc", r=[("gbcs", s)], w=["gbc"])
                if past:
                    dma("sp", hist[:], I["sconv"], "l_hist", w=["hist"])
                else:
                    P.add("dve", lambda e: e.memset(hist[:], 0.0), w=["hist"])
                NQB = min(512, T)
                ntile = (NQB + 127) // 128

                def _front(q0, info):
                    mx, kmx = mxr.next()
                    x2b, kx2 = x2r.next()
                    h2T, kh2 = h2r.next()
                    info.update(x2b=x2b, kx2=kx2, h2T=h2T, kh2=kh2)
                    dma("sp", mx[:, :, 0:NQB], sq["mxs"].rearrange("(c p) t -> p c t", p=128)[:, :, q0:q0 + NQB], "l_mx%d" % kmx[1],
                        r=[("mxs", sq["name"], q0)], w=[kmx])
                    yield

                    def _p2tile(it):
                        t0 = q0 + it * 128
                        tn = min(128, T - t0)
                        col = it * 128
                        xt, kx = xr.next()
                        xs_, kxs = xsr.next()
                        s8, ks8 = st8.next()
                        dma("sp", xt[0:tn, :], sq["x"][t0:t0 + tn, :], "l_xx%d" % kx[1], w=[kx])
                        for half in range(2):
                            pm, km, im = PS.get()
                            tm, ktm = tmpr.next()

                            def mmw(e, pm=pm, half=half, mx=mx, col=col, tn=tn):
                                for c in range(8):
                                    ins = e.matmul(pm[0:tn, :], lhsT=mx[:, c, col:col + tn], rhs=woutb[:, c, half * 512:(half + 1) * 512],
                                                   start=(c == 0), stop=(c == 7))
                                return ins
                            P.add("pe", mmw, r=[kmx, "woutb"], w=[km])
                            P.add("dve", lambda e, pm=pm, tm=tm, half=half, tn=tn: e.tensor_tensor(
                                out=tm[0:tn, :], in0=pm[0:tn, :], in1=gbc[0:tn, half * 512:(half + 1) * 512], op=ALU.mult),
                                r=["gbc"], w=[km, ktm])
                            PS.rel(im)
                            P.add("dve", lambda e, tm=tm, half=half, tn=tn, it=it, xt=xt: e.tensor_tensor(
                                out=x2b[0:tn, it, half * 512:(half + 1) * 512], in0=tm[0:tn, :], in1=xt[0:tn, half * 512:(half + 1) * 512],
                                op=ALU.add), r=[ktm, kx], w=[(kx2, it)])
                            yield
                        P.add("act", lambda e, it=it, tn=tn, s8=s8: e.activation(out=junk[0:tn, :], in_=x2b[0:tn, it, :], func=AF.Square,
                                                                                 accum_out=s8[0:tn, 0:1]), r=[(kx2, it)], w=["junk2", ks8])
                        P.add("dve", lambda e, tn=tn, s8=s8: e.tensor_scalar(out=s8[0:tn, 0:1], in0=s8[0:tn, 0:1], scalar1=1.0 / D, scalar2=EPS,
                                                                             op0=ALU.mult, op1=ALU.add), w=[ks8])
                        P.add("pool", lambda e, tn=tn, s8=s8: e.tensor_tensor(out=s8[0:tn, 0:1], in0=s8[0:tn, 0:1], in1=nhalf[0:tn, 0:1], op=ALU.pow),
                              r=["nhalf"], w=[ks8])
                        yield
                        P.add("act", lambda e, it=it, tn=tn, s8=s8, xs_=xs_: e.activation(out=xs_[0:tn, :], in_=x2b[0:tn, it, :], func=AF.Identity,
                                                                                          scale=s8[0:tn, 0:1]), r=[(kx2, it), ks8], w=[kxs])
                        yield
                        yield
                        yield
                        pb, kb, ib = PS.get()
                        pbv = bf(pb[:, :])

                        def tr(e, pbv=pbv, xs_=xs_, tn=tn):
                            for c in range(8):
                                ins = e.transpose(out=pbv[:, c * tn:(c + 1) * tn], in_=xs_[0:tn, c * 128:(c + 1) * 128], identity=identb[0:tn, 0:tn])
                            return ins
                        P.add("pe", tr, r=[kxs, "identb"], w=[kb])
                        yield
                        yield

                        def evh(e, pbv=pbv, tn=tn, col=col, s=s):
                            for c in range(8):
                                ins = e.tensor_scalar(out=h2T[:, c, col:col + tn], in0=pbv[:, c * tn:(c + 1) * tn], scalar1=mul2[:, s, c:c + 1],
                                                      scalar2=mods[:, 24 + c, s:s + 1], op0=ALU.mult, op1=ALU.add)
                            return ins
                        P.add("dve", evh, r=["mods", ("mul", id(mul2), s)], w=[kb, kh2])
                        PS.rel(ib)
                        yield
                    for it in range(ntile):
                        yield from _p2tile(it)

                def _p2blk(q0, info, bg):
                    x2b, kx2, h2T, kh2 = info["x2b"], info["kx2"], info["h2T"], info["kh2"]
                    nopp = 3 * NJ
                    tick = [0, 0]

                    def pump():
                        if bg is None:
                            return
                        tick[0] += 1
                        want = (tick[0] * 44) // nopp
                        while tick[1] < want:
                            tick[1] += 1
                            try:
                                next(bg)
                            except StopIteration:
                                return

                    def _c1(j):
                        wug, kwug = war.next()
                        dma("sp", wug[:, 0, :, :], wups[j], "l_wu%d" % kwug[1], r=[("wups", c) for c in range(8)], w=[kwug])
                        dma("sp", wug[:, 1, :, :], wups[NJ + j], "l_wg%d" % kwug[1], r=[("wups", c) for c in range(8)], w=[kwug])
                        pa, ka, ia = PS.get()
                        pg, kg_, ig = PS.get()
                        for (pp_, kk_, wi) in ((pa, ka, 0), (pg, kg_, 1)):
                            def mmf(e, pp_=pp_, wi=wi, wug=wug):
                                for c in range(8):
                                    ins = e.matmul(pp_[:, 0:NQB], lhsT=wug[:, wi, c, :], rhs=h2T[:, c, 0:NQB], start=(c == 0), stop=(c == 7))
                                return ins
                            P.add("pe", mmf, r=[kwug, kh2], w=[kk_])
                        ab, kab = abr.next()
                        cb, kcb = cbr.next()
                        gl, kgl = glr.next()
                        P.add("dve", lambda e, ab=ab, j=j: e.tensor_copy(out=ab[:, 0:2], in_=hist[:, j, :]), r=[("hist", j), "hist"], w=[kab])
                        P.add("act", lambda e, ab=ab, pa=pa: e.activation(out=ab[:, 2:2 + NQB], in_=pa[:, 0:NQB], func=AF.Copy), w=[ka, kab])
                        P.add("dve", lambda e, ab=ab, j=j: e.tensor_copy(out=hist[:, j, :], in_=ab[:, NQB:NQB + 2]), r=[kab], w=[("hist", j)])
                        P.add("act", lambda e, pa=pa, cb=cb, j=j: e.activation(out=cb[:, 0:NQB], in_=pa[:, 0:NQB], func=AF.Identity,
                                                                               scale=wcv[:, j, 2:3], bias=bcv[:, j:j + 1]),
                              r=["wcv", "bcv"], w=[ka, kcb])
                        PS.rel(ia)
                        P.add("dve", lambda e, ab=ab, cb=cb, j=j: e.scalar_tensor_tensor(out=cb[:, 0:NQB], in0=ab[:, 1:1 + NQB], scalar=wcv[:, j, 1:2],
                                                                                         in1=cb[:, 0:NQB], op0=ALU.mult, op1=ALU.add),
                              r=[kab, "wcv"], w=[kcb])
                        P.add("dve", lambda e, ab=ab, cb=cb, j=j: e.scalar_tensor_tensor(out=cb[:, 0:NQB], in0=ab[:, 0:NQB], scalar=wcv[:, j, 0:1],
                                                                                         in1=cb[:, 0:NQB], op0=ALU.mult, op1=ALU.add),
                              r=[kab, "wcv"], w=[kcb])
                        P.add("act", lambda e, cb=cb, gl=gl: e.activation(out=gl[:, 0:NQB], in_=cb[:, 0:NQB], func=AF.Gelu_apprx_tanh), r=[kcb], w=[kgl])
                        P.add("dve", lambda e, gl=gl, pg=pg, j=j: e.tensor_tensor(out=uT[:, j, 0:NQB], in0=pg[:, 0:NQB], in1=gl[:, 0:NQB], op=ALU.mult),
                              r=[kgl], w=[kg_, ("uT", j)])
                        PS.rel(ig)
                    for j in range(NJ):
                        _c1(j)
                        pump()
                    def _c2(half):
                        yb4 = [PS.get() for _ in range(ntile)]
                        for j in range(NJ):
                            wd, kwd = wdr.next()
                            dma("sp", wd[:], wdowns[j * 128:(j + 1) * 128, half * 512:(half + 1) * 512], "l_wd%d" % kwd[1], r=["wdowns"], w=[kwd])
                            for it in range(ntile):
                                tn = min(128, T - (q0 + it * 128))
                                P.add("pe", lambda e, it=it, tn=tn, j=j, wd=wd: e.matmul(
                                    yb4[it][0][0:tn, :], lhsT=uT[:, j, it * 128:it * 128 + tn], rhs=wd[:, :], start=(j == 0), stop=(j == NJ - 1)),
                                    r=[("uT", j), kwd], w=[yb4[it][1]])
                            pump()
                        for it in range(ntile):
                            tn = min(128, T - (q0 + it * 128))
                            tm, ktm = tmpr.next()
                            P.add("dve", lambda e, it=it, tn=tn, tm=tm, half=half: e.tensor_tensor(
                                out=tm[0:tn, :], in0=yb4[it][0][0:tn, :], in1=gbc[0:tn, 1024 + half * 512:1024 + (half + 1) * 512], op=ALU.mult),
                                r=["gbc"], w=[yb4[it][1], ktm])
                            PS.rel(yb4[it][2])
                            P.add("dve", lambda e, it=it, tn=tn, tm=tm, half=half: e.tensor_tensor(
                                out=x2b[0:tn, it, half * 512:(half + 1) * 512], in0=tm[0:tn, :], in1=x2b[0:tn, it, half * 512:(half + 1) * 512],
                                op=ALU.add), r=[ktm], w=[(kx2, it)])
                    for half in range(2):
                        _c2(half)
                    for it in range(ntile):
                        t0 = q0 + it * 128
                        tn = min(128, T - t0)
                        dma(XQ, sq["y"][t0:t0 + tn, :], x2b[0:tn, it, :], "o_y%d_%d" % (kx2[1], it), r=[(kx2, it)])
                    if bg is not None:
                        for _ in bg:
                            pass
                blocks = list(range(0, T, NQB))
                infos = [dict() for _ in blocks]
                for _ in _front(blocks[0], infos[0]):
                    pass
                for bi, q0 in enumerate(blocks):
                    bg = _front(blocks[bi + 1], infos[bi + 1]) if bi + 1 < len(blocks) else None
                    _p2blk(q0, infos[bi], bg)
                P.add("dve", lambda e: e.tensor_copy(out=cvo[:].rearrange("p (r j) -> p r j", r=2), in_=hist[:].rearrange("p j r -> p r j")),
                      r=["hist"] + [("hist", j) for j in range(NJ)], w=["cvo"])
                pb, kb, ib = PS.get()
                P.add("pe", lambda e, pb=pb: e.transpose(out=pb[0:2 * NJ, 0:128], in_=cvo[:, :], identity=identf[:, :]), r=["cvo", "identf"], w=[kb])
                P.add("act", lambda e, pb=pb: e.activation(out=cvt[:, :], in_=pb[0:2 * NJ, 0:128], func=AF.Copy), w=[kb, "cvt"])
                PS.rel(ib)
                for r_ in range(2):
                    dma("sp", sq["convo"][r_].rearrange("(j p) -> j p", p=128), cvt[r_ * NJ:(r_ + 1) * NJ, :], "o_conv%d" % r_, r=["cvt"])
            for sq in seqs[:max(0, STAGE - 3)]:
                _p2seq(sq)
            P.emit(nc, semget)
        P.emit(nc, semget, final=True)
    return nc


_CACHE = {}


def _consts(TP):
    npos = max(TP, PAST + TS)
    half = 16
    inv = (1.0 / (np.float32(10000.0) ** (np.arange(half, dtype=np.float32) / np.float32(half)))).astype(np.float32)
    ang = np.arange(npos, dtype=np.float32)[:, None] * inv[None, :]
    cos, sin = np.cos(ang).astype(np.float32), np.sin(ang).astype(np.float32)
    tab = np.concatenate([cos, cos, -sin, sin], axis=1).astype(np.float32)
    ident = np.eye(128, dtype=np.float32)
    tri = np.triu(np.ones((128, 128), np.float32))
    ones = np.ones((128, 128), np.float32)
    sel2 = np.zeros((2, 2, 128), np.float32)
    sel2[0, 0, :] = 1.0
    sel2[1, 1, :] = 1.0
    return dict(tab=tab, ident=ident, tri=tri, ones=ones, sel2=sel2)


def kernel(x_prompt, x_sample, c_prompt, c_sample, cache_ckv, cache_kpe, state_gla, state_ffn_conv,
           w_ada, b_ada, g_norm1, w_in, g_qa, w_uq, g_qn, g_kva, w_ukv, g_kn, w_a2, b_a2, g_gla,
           w_out, g_norm2, w_up, w_conv, b_conv, w_down):
    f = lambda a: np.ascontiguousarray(np.asarray(a, dtype=np.float32))
    x_prompt = f(x_prompt)
    B, TP, _ = x_prompt.shape
    import os
    if TP not in _CACHE:
        _CACHE[TP] = build(TP, int(os.environ.get('KSTAGE', '9')))
    nc = _CACHE[TP]
    cs = _consts(TP)
    fm = lambda v, n: f(np.asarray(v, np.float32).reshape(n, 128).T)
    b_ada0 = np.asarray(b_ada, np.float32)[0]
    shared = dict(
        wada=f(w_ada[0]), bada_fm=fm(b_ada0, 48),
        bada2=f(np.tile(np.concatenate([b_ada0[2048:3072], b_ada0[5120:6144]])[None, :], (2, 1))),
        g1_fm=fm(g_norm1[0], 8), g2_fm=fm(g_norm2[0], 8), win=f(w_in[0]), gqa_fm=fm(g_qa[0], 3), wuq=f(w_uq[0]),
        gqn8=f(np.tile(np.asarray(g_qn[0], np.float32), 8)[None, :]), gkva=f(np.asarray(g_kva[0])[None, :]), wukv=f(w_ukv[0]),
        gkn=f(np.asarray(g_kn[0])[None, :]), wa2=f(w_a2[0]), ba2=f(np.asarray(b_a2[0])[None, :]), ggla_fm=fm(g_gla[0], 1),
        wout=f(w_out[0]), wup=f(w_up[0]),
        wconv_fm=f(np.asarray(w_conv[0], np.float32).reshape(3, NJ, 128).transpose(2, 1, 0)),
        bconv_fm=fm(b_conv[0], NJ), wdown=f(w_down[0]),
        ident=cs["ident"], tri=cs["tri"], ones=cs["ones"], tab=cs["tab"], sel2=cs["sel2"])
    in_maps = []
    for b in range(B):
        c2 = np.stack([np.asarray(c_prompt[b], np.float32), np.asarray(c_sample[b], np.float32)], axis=-1)
        m = dict(shared)
        m.update(
            xp=x_prompt[b], xs=f(x_sample[b]), cT=f(c2.reshape(8, 128, 2).transpose(1, 0, 2)),
            cckv=f(cache_ckv[0, b]), ckpe=f(cache_kpe[0, b]), sgla=f(state_gla[0, b]),
            sconv=f(np.asarray(state_ffn_conv[0, b], np.float32).reshape(2, NJ, 128).transpose(2, 1, 0)))
        in_maps.append(m)
    res = run_bass_kernel_spmd(nc, in_maps, core_ids=list(range(B)))
    R = res.results
    g = lambda k: np.stack([np.asarray(R[b][k], np.float32) for b in range(B)], axis=0)
    return (g("yp"), g("ys"), g("ckvp")[None], g("kpep")[None], g("glap")[None], g("convp")[None],
            g("ckvs")[None], g("kpes")[None], g("glas")[None], g("convs")[None])
```
